# Optimizing a Trainium2 kernel written in Bass

```python
import math
import jax, jax.numpy as jnp
from jax import lax
import numpy as np

D_MODEL = 2048
BATCH = 2
SEQ = 4096
DEPTH = 4

N_MIXERS = 3
N_S5 = (DEPTH + 2) // 3
N_SSD = (DEPTH + 1) // 3
N_RET = DEPTH // 3

DN_ALPHA = (2 * DEPTH) ** 0.25
DN_BETA = (8 * DEPTH) ** -0.25
LN_EPS = 1e-5

D_FF = 4 * D_MODEL

S5_GROUP = 16
S5_GROUPS = D_MODEL // S5_GROUP
S5_STATE = 64
S5_DT_MIN = 1e-3
S5_DT_MAX = 1e-1

SSD_EXPAND = 2
SSD_D_INNER = SSD_EXPAND * D_MODEL
SSD_HEADDIM = 64
SSD_HEADS = SSD_D_INNER // SSD_HEADDIM
SSD_GROUPS = 8
SSD_STATE = 128
SSD_CONV = 4
SSD_CHUNK = 128
SSD_CONV_DIM = SSD_D_INNER + 2 * SSD_GROUPS * SSD_STATE
SSD_IN_DIM = SSD_D_INNER + SSD_CONV_DIM + SSD_HEADS

RET_HEADS = 8
RET_DK = D_MODEL // RET_HEADS
RET_DV = 2 * D_MODEL // RET_HEADS
RET_CHUNK = 128
RET_IN_DIM = 2 * D_MODEL + 4 * D_MODEL
RET_ROPE_BASE = 10000.0

kernel_name = 'hybrid_s5_ssd_retention_deepnorm'


def _layer_norm(x, g, b):
    xf = x.astype(jnp.float32)
    mu = jnp.mean(xf, axis=-1, keepdims=True)
    var = jnp.mean(jnp.square(xf - mu), axis=-1, keepdims=True)
    return ((xf - mu) * lax.rsqrt(var + LN_EPS)).astype(x.dtype) * g + b


def _mlp(x, w1, w2):
    h = jnp.square(jax.nn.relu(x @ w1))
    return h @ w2


def _s5_combine(left, right):
    ar1, ai1, br1, bi1 = left
    ar2, ai2, br2, bi2 = right
    return (ar2 * ar1 - ai2 * ai1,
            ar2 * ai1 + ai2 * ar1,
            ar2 * br1 - ai2 * bi1 + br2,
            ar2 * bi1 + ai2 * br1 + bi2)


def _s5_mixer(x, w_in, lam_re, lam_im, log_dt, b_re, b_im, c_re, c_im, d_skip, w_out, w_gate):
    bsz, seq, _ = x.shape
    f32 = jnp.float32
    u = (x @ w_in).astype(f32)
    ug = u.reshape(bsz, seq, S5_GROUPS, S5_GROUP)
    lr = lam_re.astype(f32)
    li = lam_im.astype(f32)
    dt = jnp.exp(log_dt.astype(f32))[:, None]
    mag = jnp.exp(lr * dt)
    ar = mag * jnp.cos(li * dt)
    ai = mag * jnp.sin(li * dt)
    den = lr * lr + li * li
    zr = ((ar - 1.0) * lr + ai * li) / den
    zi = (ai * lr - (ar - 1.0) * li) / den
    br_ = b_re.astype(f32)
    bi_ = b_im.astype(f32)
    bbr = zr[..., None] * br_ - zi[..., None] * bi_
    bbi = zr[..., None] * bi_ + zi[..., None] * br_
    bu_r = jnp.einsum('blgc,gpc->blgp', ug, bbr)
    bu_i = jnp.einsum('blgc,gpc->blgp', ug, bbi)
    a_r = jnp.broadcast_to(ar, (1, seq) + ar.shape)
    a_i = jnp.broadcast_to(ai, (1, seq) + ai.shape)
    _, _, s_r, s_i = lax.associative_scan(_s5_combine, (a_r, a_i, bu_r, bu_i), axis=1)
    y = (jnp.einsum('blgp,gcp->blgc', s_r, c_re.astype(f32))
         - jnp.einsum('blgp,gcp->blgc', s_i, c_im.astype(f32)))
    y = y.reshape(bsz, seq, D_MODEL) + d_skip.astype(f32) * u
    h = jax.nn.gelu(y).astype(x.dtype)
    return (h @ w_out) * jax.nn.sigmoid(h @ w_gate)


def _causal_depthwise_conv(x, w, b):
    rhs = w.astype(x.dtype)[:, None, :]
    out = lax.conv_general_dilated(x, rhs, window_strides=(1,), padding=[(SSD_CONV - 1, 0)],
                                   dimension_numbers=('NWC', 'WIO', 'NWC'),
                                   feature_group_count=x.shape[-1])
    return out + b


def _ssd_chunked(xdt, adt, bm, cm):
    bsz, seq = xdt.shape[:2]
    nc = seq // SSD_CHUNK
    r = SSD_HEADS // SSD_GROUPS
    xc = xdt.reshape(bsz, nc, SSD_CHUNK, SSD_GROUPS, r, SSD_HEADDIM)
    bc = bm.reshape(bsz, nc, SSD_CHUNK, SSD_GROUPS, SSD_STATE)
    cc = cm.reshape(bsz, nc, SSD_CHUNK, SSD_GROUPS, SSD_STATE)
    a = adt.reshape(bsz, nc, SSD_CHUNK, SSD_GROUPS, r).transpose(0, 3, 4, 1, 2)
    a_cs = jnp.cumsum(a, axis=-1)
    causal = jnp.tril(jnp.ones((SSD_CHUNK, SSD_CHUNK), dtype=bool))
    seg = a_cs[..., :, None] - a_cs[..., None, :]
    lmat = jnp.exp(jnp.where(causal, seg, -jnp.inf))
    cb = jnp.einsum('bclgn,bcsgn->bgcls', cc, bc)
    y_diag = jnp.einsum('bgrcls,bcsgrp->bclgrp', cb[:, :, None] * lmat, xc)
    decay_to_end = jnp.exp(a_cs[..., -1:] - a_cs).transpose(0, 3, 4, 1, 2)
    states = jnp.einsum('bcsgn,bcsgrp->bcgrpn', bc, xc * decay_to_end[..., None])
    tot = a_cs[..., -1]
    cum_tot = jnp.cumsum(tot, axis=-1)
    excl = cum_tot - tot
    strict = jnp.tril(jnp.ones((nc, nc), dtype=bool), -1)
    m = jnp.exp(jnp.where(strict, excl[..., :, None] - cum_tot[..., None, :], -jnp.inf))
    enter = jnp.einsum('bgrzc,bcgrpn->bzgrpn', m, states)
    decay_in = jnp.exp(a_cs).transpose(0, 3, 4, 1, 2)
    y_off = jnp.einsum('bclgn,bcgrpn->bclgrp', cc, enter) * decay_in[..., None]
    return (y_diag + y_off).reshape(bsz, seq, SSD_HEADS, SSD_HEADDIM)


def _ssd_mixer(x, w_in, conv_w, conv_b, dt_bias, a_log, d_skip, norm_w, w_out):
    bsz, seq, _ = x.shape
    f32 = jnp.float32
    proj = x @ w_in
    z, xbc, dt = jnp.split(proj, [SSD_D_INNER, SSD_D_INNER + SSD_CONV_DIM], axis=-1)
    xbc = jax.nn.silu(_causal_depthwise_conv(xbc, conv_w, conv_b)).astype(f32)
    xs, bm, cm = jnp.split(xbc, [SSD_D_INNER, SSD_D_INNER + SSD_GROUPS * SSD_STATE], axis=-1)
    xs = xs.reshape(bsz, seq, SSD_HEADS, SSD_HEADDIM)
    bm = bm.reshape(bsz, seq, SSD_GROUPS, SSD_STATE)
    cm = cm.reshape(bsz, seq, SSD_GROUPS, SSD_STATE)
    dt = jax.nn.softplus(dt.astype(f32) + dt_bias.astype(f32))
    a = -jnp.exp(a_log.astype(f32))
    y = _ssd_chunked(xs * dt[..., None], dt * a, bm, cm)
    y = y + d_skip.astype(f32)[:, None] * xs
    y = y.reshape(bsz, seq, SSD_D_INNER) * jax.nn.silu(z.astype(f32))
    yg = y.reshape(bsz, seq, SSD_GROUPS, SSD_D_INNER // SSD_GROUPS)
    yg = yg * lax.rsqrt(jnp.mean(jnp.square(yg), axis=-1, keepdims=True) + LN_EPS)
    y = yg.reshape(bsz, seq, SSD_D_INNER) * norm_w.astype(f32)
    return y.astype(x.dtype) @ w_out


def _rotate(t, cos, sin):
    t1, t2 = jnp.split(t, 2, axis=-1)
    c = cos[None, :, None, :]
    s = sin[None, :, None, :]
    return jnp.concatenate([t1 * c - t2 * s, t1 * s + t2 * c], axis=-1)


def _retention_mixer(x, w_in, w_out):
    bsz, seq, _ = x.shape
    f32 = jnp.float32
    proj = x @ w_in
    q, k, v, g = jnp.split(proj, [D_MODEL, 2 * D_MODEL, 4 * D_MODEL], axis=-1)
    q = q.astype(f32).reshape(bsz, seq, RET_HEADS, RET_DK)
    k = k.astype(f32).reshape(bsz, seq, RET_HEADS, RET_DK) * (RET_DK ** -0.5)
    v = v.astype(f32).reshape(bsz, seq, RET_HEADS, RET_DV)
    theta = 1.0 / (RET_ROPE_BASE ** jnp.linspace(0.0, 1.0, RET_DK // 2, dtype=f32))
    ang = jnp.arange(seq, dtype=f32)[:, None] * theta[None, :]
    cos, sin = jnp.cos(ang), jnp.sin(ang)
    q = _rotate(q, cos, sin)
    k = _rotate(k, cos, sin)
    lg = jnp.log1p(-jnp.exp2(-5.0 - jnp.arange(RET_HEADS, dtype=f32)))
    pos = jnp.arange(RET_CHUNK, dtype=f32)
    diff = pos[:, None] - pos[None, :]
    dmat = jnp.where(diff >= 0, jnp.exp(jnp.maximum(diff, 0.0)[None] * lg[:, None, None]), 0.0)
    xi = jnp.exp((pos[None, :] + 1.0) * lg[:, None])
    zeta = jnp.exp((RET_CHUNK - 1.0 - pos[None, :]) * lg[:, None])
    chunk_decay = jnp.exp(RET_CHUNK * lg)
    nc = seq // RET_CHUNK
    def to_chunks(t):
        return t.reshape(bsz, nc, RET_CHUNK, RET_HEADS, t.shape[-1]).transpose(1, 0, 3, 2, 4)
    qs, ks, vs = to_chunks(q), to_chunks(k), to_chunks(v)
    def step(state, qkv):
        qc, kc, vc = qkv
        scores = jnp.einsum('bhnd,bhmd->bhnm', qc, kc) * dmat
        inner = jnp.einsum('bhnm,bhme->bhne', scores, vc)
        cross = jnp.einsum('bhnd,bhde->bhne', qc, state) * xi[:, :, None]
        state = (chunk_decay[:, None, None] * state
                 + jnp.einsum('bhmd,bhme->bhde', kc * zeta[:, :, None], vc))
        return state, inner + cross
    state0 = jnp.zeros((bsz, RET_HEADS, RET_DK, RET_DV), f32)
    _, out = lax.scan(step, state0, (qs, ks, vs))
    out = out.transpose(1, 0, 3, 2, 4).reshape(bsz, seq, RET_HEADS, RET_DV)
    mu = jnp.mean(out, axis=-1, keepdims=True)
    var = jnp.mean(jnp.square(out - mu), axis=-1, keepdims=True)
    out = ((out - mu) * lax.rsqrt(var + LN_EPS)).reshape(bsz, seq, 2 * D_MODEL)
    y = jax.nn.silu(g.astype(f32)) * out
    return y.astype(x.dtype) @ w_out


def setup_inputs(seed: int = 0) -> dict:
    key = jax.random.key(seed)
    ks = iter(jax.random.split(key, 32))
    f32 = jnp.float32
    def nrm(shape, scale):
        return jax.random.normal(next(ks), shape, f32) * scale
    def unif(shape, lo, hi):
        return jax.random.uniform(next(ks), shape, f32, lo, hi)
    x = nrm((BATCH, SEQ, D_MODEL), 1.0)
    ln1_g = 1.0 + nrm((DEPTH, D_MODEL), 0.02)
    ln1_b = nrm((DEPTH, D_MODEL), 0.02)
    ln2_g = 1.0 + nrm((DEPTH, D_MODEL), 0.02)
    ln2_b = nrm((DEPTH, D_MODEL), 0.02)
    mlp_w1 = nrm((DEPTH, D_MODEL, D_FF), D_MODEL ** -0.5)
    mlp_w2 = nrm((DEPTH, D_FF, D_MODEL), DN_BETA * D_FF ** -0.5)
    s5_w_in = nrm((N_S5, D_MODEL, D_MODEL), D_MODEL ** -0.5)
    s5_lam_re = -0.5 + nrm((N_S5, S5_GROUPS, S5_STATE), 0.01)
    s5_lam_im = (jnp.pi * jnp.arange(S5_STATE, dtype=f32)) + nrm((N_S5, S5_GROUPS, S5_STATE), 0.01)
    s5_log_dt = unif((N_S5, S5_GROUPS), math.log(S5_DT_MIN), math.log(S5_DT_MAX))
    s5_b_re = nrm((N_S5, S5_GROUPS, S5_STATE, S5_GROUP), (2 * S5_GROUP) ** -0.5)
    s5_b_im = nrm((N_S5, S5_GROUPS, S5_STATE, S5_GROUP), (2 * S5_GROUP) ** -0.5)
    s5_c_re = nrm((N_S5, S5_GROUPS, S5_GROUP, S5_STATE), (2 * S5_STATE) ** -0.5)
    s5_c_im = nrm((N_S5, S5_GROUPS, S5_GROUP, S5_STATE), (2 * S5_STATE) ** -0.5)
    s5_d = nrm((N_S5, D_MODEL), 1.0)
    s5_w_out = nrm((N_S5, D_MODEL, D_MODEL), DN_BETA * D_MODEL ** -0.5)
    s5_w_gate = nrm((N_S5, D_MODEL, D_MODEL), D_MODEL ** -0.5)
    ssd_w_in = nrm((N_SSD, D_MODEL, SSD_IN_DIM), D_MODEL ** -0.5)
    ssd_conv_w = nrm((N_SSD, SSD_CONV, SSD_CONV_DIM), SSD_CONV ** -0.5)
    ssd_conv_b = nrm((N_SSD, SSD_CONV_DIM), 0.02)
    ssd_dt0 = jnp.exp(unif((N_SSD, SSD_HEADS), math.log(1e-3), math.log(1e-1)))
    ssd_dt_bias = ssd_dt0 + jnp.log(-jnp.expm1(-ssd_dt0))
    ssd_a_log = jnp.log(unif((N_SSD, SSD_HEADS), 1.0, 16.0))
    ssd_d = 1.0 + nrm((N_SSD, SSD_HEADS), 0.1)
    ssd_norm_w = 1.0 + nrm((N_SSD, SSD_D_INNER), 0.02)
    ssd_w_out = nrm((N_SSD, SSD_D_INNER, D_MODEL), DN_BETA * SSD_D_INNER ** -0.5)
    ret_w_in = nrm((N_RET, D_MODEL, RET_IN_DIM), D_MODEL ** -0.5)
    ret_w_out = nrm((N_RET, 2 * D_MODEL, D_MODEL), DN_BETA * (2 * D_MODEL) ** -0.5)
    return {'x': x, 'ln1_g': ln1_g, 'ln1_b': ln1_b, 'ln2_g': ln2_g, 'ln2_b': ln2_b,
            'mlp_w1': mlp_w1, 'mlp_w2': mlp_w2,
            's5_w_in': s5_w_in, 's5_lam_re': s5_lam_re, 's5_lam_im': s5_lam_im, 's5_log_dt': s5_log_dt,
            's5_b_re': s5_b_re, 's5_b_im': s5_b_im, 's5_c_re': s5_c_re, 's5_c_im': s5_c_im,
            's5_d': s5_d, 's5_w_out': s5_w_out, 's5_w_gate': s5_w_gate,
            'ssd_w_in': ssd_w_in, 'ssd_conv_w': ssd_conv_w, 'ssd_conv_b': ssd_conv_b,
            'ssd_dt_bias': ssd_dt_bias, 'ssd_a_log': ssd_a_log, 'ssd_d': ssd_d,
            'ssd_norm_w': ssd_norm_w, 'ssd_w_out': ssd_w_out,
            'ret_w_in': ret_w_in, 'ret_w_out': ret_w_out}


def reference(x, ln1_g, ln1_b, ln2_g, ln2_b, mlp_w1, mlp_w2,
              s5_w_in, s5_lam_re, s5_lam_im, s5_log_dt, s5_b_re, s5_b_im, s5_c_re, s5_c_im,
              s5_d, s5_w_out, s5_w_gate,
              ssd_w_in, ssd_conv_w, ssd_conv_b, ssd_dt_bias, ssd_a_log, ssd_d, ssd_norm_w, ssd_w_out,
              ret_w_in, ret_w_out):
    for i in range(DEPTH):
        kind = i % N_MIXERS
        j = i // N_MIXERS
        if kind == 0:
            f = _s5_mixer(x, s5_w_in[j], s5_lam_re[j], s5_lam_im[j], s5_log_dt[j], s5_b_re[j], s5_b_im[j],
                          s5_c_re[j], s5_c_im[j], s5_d[j], s5_w_out[j], s5_w_gate[j])
        elif kind == 1:
            f = _ssd_mixer(x, ssd_w_in[j], ssd_conv_w[j], ssd_conv_b[j], ssd_dt_bias[j], ssd_a_log[j],
                           ssd_d[j], ssd_norm_w[j], ssd_w_out[j])
        else:
            f = _retention_mixer(x, ret_w_in[j], ret_w_out[j])
        x = _layer_norm(DN_ALPHA * x + f.astype(x.dtype), ln1_g[i], ln1_b[i])
        x = _layer_norm(DN_ALPHA * x + _mlp(x, mlp_w1[i], mlp_w2[i]).astype(x.dtype), ln2_g[i], ln2_b[i])
    return x
```

```python
import math
from contextlib import ExitStack
import numpy as np
import concourse.bass as bass
import concourse.mybir as mybir
from concourse.bass_utils import run_bass_kernel_spmd

F32 = mybir.dt.float32
BF16 = mybir.dt.bfloat16
I32 = mybir.dt.int32
AF = mybir.ActivationFunctionType
ALU = mybir.AluOpType
AX = mybir.AxisListType

TWO_PI = 2.0 * math.pi
MAGIC = 12582912.0


class T:
    __slots__ = ("name", "w", "r", "lo", "hi")

    def __init__(self, name):
        self.name = name
        self.w = None
        self.r = {}
        self.lo = self.hi = 0


class K:
    ENG = ("pe", "act", "dve", "pool", "sp")

    def __init__(self, nc, n_dma_sems=8):
        self.nc = nc
        self.ops = []
        self.n_dma_sems = n_dma_sems

    def _rec(self, eng, kind, fn, reads, writes):
        idx = len(self.ops)
        deps = set()
        for t in reads:
            if t.w is not None:
                deps.add(t.w)
        for t in writes:
            if t.w is not None:
                deps.add(t.w)
            deps.update(t.r.values())
        deps.discard(idx)
        if eng == "pe" and kind == "c":
            deps = {d for d in deps if not (self.ops[d]["eng"] == "pe" and self.ops[d]["kind"] == "c")}
        self.ops.append(dict(eng=eng, kind=kind, fn=fn, deps=deps, sig=(kind == "d")))
        for d in deps:
            self.ops[d]["sig"] = True
        for t in writes:
            t.w = idx
            t.r = {}
        rk = eng if kind == "c" else ("d", idx)
        for t in reads:
            if t in writes:
                continue
            t.r[rk] = idx
        return idx

    def op(self, eng, fn, reads=(), writes=()):
        return self._rec(eng, "c", fn, list(reads), list(writes))

    def dma(self, q, out, in_, reads=(), writes=(), **kw):
        return self._rec(q, "d", lambda e: e.dma_start(out=out, in_=in_, **kw), list(reads), list(writes))

    def emit(self, stack):
        nc = self.nc
        ops = self.ops
        sems = {e: stack.enter_context(nc.semaphore(f"s_{e}")) for e in ("pe", "act", "dve", "pool")}
        dsems = {}
        for q in ("sp", "act", "pool"):
            if any(o["kind"] == "d" and o["eng"] == q for o in ops):
                dsems[q] = [stack.enter_context(nc.semaphore(f"d_{q}{i}")) for i in range(self.n_dma_sems)]
        cnt = {e: 0 for e in sems}
        dcnt = {q: 0 for q in dsems}
        for o in ops:
            o["pre"] = None
            if o["kind"] == "c":
                if o["sig"]:
                    cnt[o["eng"]] += 1
                    o["tok"] = (sems[o["eng"]], cnt[o["eng"]])
            else:
                q = o["eng"]
                i = dcnt[q]
                dcnt[q] += 1
                s = dsems[q][i % self.n_dma_sems]
                r = i // self.n_dma_sems
                o["tok"] = (s, 16 * (r + 1))
                if r > 0:
                    o["pre"] = (s, 16 * r)
        streams = {e: [] for e in self.ENG}
        for i, o in enumerate(ops):
            streams[o["eng"]].append(i)
        final = {}
        for o in ops:
            if o["kind"] == "d":
                s, v = o["tok"]
                final[(o["eng"], id(s))] = (s, v)
        block = stack.enter_context(nc.Block())

        def run(ename, eng):
            waited = {}

            def wait(tok):
                s, v = tok
                if waited.get(id(s), 0) >= v:
                    return
                eng.wait_ge(s, v)
                waited[id(s)] = v

            for i in streams[ename]:
                o = ops[i]
                for d in sorted(o["deps"]):
                    wait(ops[d]["tok"])
                if o["pre"] is not None:
                    wait(o["pre"])
                ins = o["fn"](eng)
                if o["sig"]:
                    s, v = o["tok"]
                    ins.then_inc(s, 16 if o["kind"] == "d" else 1)
            for (q, _), tok in final.items():
                if q == ename:
                    wait(tok)

        @block.tensor
        def _(e):
            run("pe", e)

        @block.scalar
        def _(e):
            run("act", e)

        @block.vector
        def _(e):
            run("dve", e)

        @block.gpsimd
        def _(e):
            run("pool", e)

        @block.sync
        def _(e):
            run("sp", e)


class Cfg:
    def __init__(self, D=2048, T=1024, TT=512, L=512):
        self.D = D
        self.KC = D // 128
        self.T = T
        self.TT = TT
        self.NTT = T // TT
        self.L = L
        self.DFF = 4 * D
        self.G = D // 16
        self.alpha = 8.0 ** 0.25
        self.eps = 1e-5
        self.WG = min(512, D)


class Prog:
    ARENA_WORDS = 52600

    def __init__(self, cfg):
        self.c = cfg
        self.nc = bass.Bass("TRN2", target_bir_lowering=False)
        self.st = ExitStack()
        self.k = K(self.nc)
        nc = self.nc
        self.arena = self.st.enter_context(nc.sbuf_tensor("arena", [128, self.ARENA_WORDS], F32))
        self.top = 0
        self.live = []
        self.grave = []
        self.ps = [self.st.enter_context(nc.psum_tensor(f"ps{i}", [128, 512], F32)) for i in range(8)]
        self.tps = [T(f"ps{i}") for i in range(8)]
        self.ins = {}
        self.outs = {}
        self.wq = 0
        self.ones128, self.t_ones128 = self.alloc("ones128", [128], F32)
        self.k.op("pool", lambda e: e.memset(self.ones128, 1.0), writes=[self.t_ones128])
        self.eps128, self.t_eps128 = self.alloc("eps128", [1], F32)
        self.k.op("pool", lambda e: e.memset(self.eps128, cfg.eps), writes=[self.t_eps128])
        wwords = cfg.KC * 512 // 2
        self.wbuf = []
        for i in range(2):
            ap, t = self.alloc(f"wbuf{i}", [wwords * 2], BF16)
            self.wbuf.append((ap, t))

    def alloc(self, name, free_shape, dtype):
        n = int(np.prod(free_shape))
        words = n if dtype in (F32, I32) else (n + 1) // 2
        lo = self.top
        hi = lo + words
        assert hi <= self.ARENA_WORDS, f"SBUF arena overflow allocating {name}: {hi}"
        self.top = hi
        ap = self.arena[:, lo:hi]
        if dtype != F32:
            ap = ap.bitcast(dtype)
        if len(free_shape) == 2:
            ap = ap.rearrange("p (a b) -> p a b", a=free_shape[0])
        elif len(free_shape) == 3:
            ap = ap.rearrange("p (a b c) -> p a b c", a=free_shape[0], b=free_shape[1])
        t = T(name)
        t.lo, t.hi = lo, hi
        k = 0
        for g in self.grave:
            if g.lo < hi and lo < g.hi:
                if g.w is not None:
                    t.r[("g", k)] = g.w
                    k += 1
                for v in g.r.values():
                    t.r[("g", k)] = v
                    k += 1
        self.live.append(t)
        return ap, t

    def mark(self):
        return (self.top, len(self.live))

    def release(self, m):
        top, n = m
        self.grave.extend(self.live[n:])
        del self.live[n:]
        self.top = top

    def din(self, name, shape, dtype=F32):
        ap = self.nc.dram_tensor(name, list(shape), dtype, kind="ExternalInput").ap()
        self.ins[name] = ap
        return ap

    def dout(self, name, shape, dtype=F32):
        ap = self.nc.dram_tensor(name, list(shape), dtype, kind="ExternalOutput").ap()
        self.outs[name] = ap
        return ap

    def dbg(self, name, ap, ts, shape, dtype=F32):
        d = self.dout(name, [128] + list(shape), dtype)
        self.k.dma("sp", d, ap, reads=list(ts))

    def finish(self):
        self.k.emit(self.st)
        self.st.close()
        return self.nc

    def wload(self, W, k0, KCW, c0, ncols):
        ap, t = self.wbuf[self.wq % 2]
        self.wq += 1
        view = ap[:, 0:KCW * ncols].rearrange("p (a b) -> p a b", a=KCW)
        src = W[k0:k0 + KCW * 128, c0:c0 + ncols].rearrange("(kc p) n -> p kc n", p=128)
        self.k.dma("pool", view, src, writes=[t])
        return view, t


    def grid(self, name, n1, n2):
        return [[T(f"{name}_{i}_{j}") for j in range(n2)] for i in range(n1)]

    def alloc_act(self, name, KCn, dtype):
        c = self.c
        ap, t = self.alloc(name, [KCn, c.T], dtype)
        g = self.grid(name, KCn, c.NTT)
        for row in g:
            for x in row:
                x.lo, x.hi = t.lo, t.hi
                x.r = dict(t.r)
                self.live.append(x)
        return ap, g

    def proj_fm(self, specs, ncols, evac, banks=(0, 1, 2, 3)):
        c = self.c
        KCmax = max(s[2] for s in specs)
        cg = c.WG if KCmax <= c.KC else c.WG // (KCmax // c.KC)
        bi = 0
        for g0 in range(0, ncols, cg):
            gw = min(cg, ncols - g0)
            wl = [self.wload(W, k0, KCW, c0 + g0, gw) for (W, k0, KCW, c0, act, tg) in specs]
            for ml in range(gw // 128):
                m = (g0 // 128) + ml
                for tt in range(c.NTT):
                    pss, tpss = [], []
                    for si, (W, k0, KCW, c0, act, tg) in enumerate(specs):
                        wv, tw = wl[si]
                        b = banks[bi % len(banks)]
                        bi += 1
                        ps = self.ps[b][:, 0:c.TT]
                        for kc in range(KCW):
                            self.k.op("pe", lambda e, ps=ps, wv=wv, kc=kc, ml=ml, tt=tt, act=act, KCW=KCW: e.matmul(
                                ps, lhsT=wv[:, kc, ml * 128:(ml + 1) * 128], rhs=act[:, kc, tt * c.TT:(tt + 1) * c.TT],
                                start=(kc == 0), stop=(kc == KCW - 1)),
                                reads=[tw, tg[kc][tt]], writes=[self.tps[b]])
                        pss.append(ps)
                        tpss.append(self.tps[b])
                    evac(m, tt, pss, tpss)

    def layernorm(self, y32, tg_y, xbf, tg_xbf, g_ap, b_ap, t_gb, sb=(4, 5)):
        c = self.c
        k = self.k
        mk = self.mark()
        TT = c.TT
        sq = [self.alloc(f"ln_sq{i}", [TT], F32) for i in range(2)]
        mean, t_mean = self.alloc("ln_mean", [TT], F32)
        rstd, t_rstd = self.alloc("ln_rstd", [TT], F32)
        msq, t_msq = self.alloc("ln_msq", [TT], F32)
        tmp = [self.alloc(f"ln_tmp{i}", [TT], F32) for i in range(4)]
        invD = 1.0 / c.D
        for tt in range(c.NTT):
            sl = slice(tt * TT, (tt + 1) * TT)
            ps_s, ps_q = self.ps[sb[0]][:, 0:TT], self.ps[sb[1]][:, 0:TT]
            for kc in range(c.KC):
                sq_ap, t_sq = sq[kc % 2]
                k.op("act", lambda e, sq_ap=sq_ap, kc=kc, sl=sl: e.activation(out=sq_ap, in_=y32[:, kc, sl], func=AF.Square),
                     reads=[tg_y[kc][tt]], writes=[t_sq])
                k.op("pe", lambda e, kc=kc, sl=sl, ps_s=ps_s: e.matmul(ps_s, lhsT=self.ones128, rhs=y32[:, kc, sl],
                                                                   start=(kc == 0), stop=(kc == c.KC - 1)),
                     reads=[tg_y[kc][tt], self.t_ones128], writes=[self.tps[sb[0]]])
                k.op("pe", lambda e, kc=kc, sq_ap=sq_ap, ps_q=ps_q: e.matmul(ps_q, lhsT=self.ones128, rhs=sq_ap,
                                                                       start=(kc == 0), stop=(kc == c.KC - 1)),
                     reads=[t_sq, self.t_ones128], writes=[self.tps[sb[1]]])
            k.op("dve", lambda e, ps_s=ps_s: e.tensor_scalar(out=mean, in0=ps_s, scalar1=invD, scalar2=None, op0=ALU.mult),
                 reads=[self.tps[sb[0]]], writes=[t_mean])
            k.op("dve", lambda e: e.tensor_tensor(out=msq, in0=mean, in1=mean, op=ALU.mult), reads=[t_mean], writes=[t_msq])
            k.op("dve", lambda e, ps_q=ps_q: e.scalar_tensor_tensor(out=rstd, in0=ps_q, scalar=invD, in1=msq,
                                                                  op0=ALU.mult, op1=ALU.subtract),
                 reads=[self.tps[sb[1]], t_msq], writes=[t_rstd])
            k.op("act", lambda e: e.activation(out=rstd, in_=rstd, func=AF.Ln, bias=self.eps128),
                 reads=[t_rstd, self.t_eps128], writes=[t_rstd])
            k.op("act", lambda e: e.activation(out=rstd, in_=rstd, func=AF.Exp, scale=-0.5), reads=[t_rstd], writes=[t_rstd])
            for kc in range(c.KC):
                ta, t_ta = tmp[kc % 4]
                eng = "dve" if kc % 2 == 0 else "pool"
                k.op(eng, lambda e, ta=ta, kc=kc, sl=sl: e.tensor_tensor(out=ta, in0=y32[:, kc, sl], in1=mean, op=ALU.subtract),
                     reads=[tg_y[kc][tt], t_mean], writes=[t_ta])
                k.op("dve", lambda e, ta=ta: e.tensor_tensor(out=ta, in0=ta, in1=rstd, op=ALU.mult),
                     reads=[t_ta, t_rstd], writes=[t_ta])
                k.op("act", lambda e, ta=ta, kc=kc, sl=sl: e.activation(out=y32[:, kc, sl], in_=ta, func=AF.Identity,
                                                                      scale=g_ap[:, kc:kc + 1], bias=b_ap[:, kc:kc + 1]),
                     reads=[t_ta, t_gb], writes=[tg_y[kc][tt]])
                k.op("pool", lambda e, kc=kc, sl=sl: e.tensor_copy(out=xbf[:, kc, sl], in_=y32[:, kc, sl]),
                     reads=[tg_y[kc][tt]], writes=[tg_xbf[kc][tt]])
        self.release(mk)

    def mlp(self, x32, tg_x, xbf, tg_xbf, w1, w2):
        c = self.c
        k = self.k
        mk = self.mark()
        hbf, tg_h = self.alloc_act("mlp_h", c.KC, BF16)
        rl = [self.alloc(f"mlp_r{i}", [c.TT], F32) for i in range(2)]
        nq = c.DFF // c.D
        cnt = [0]
        for q in range(nq):
            def evac1(m, tt, pss, tpss):
                ps, tp = pss[0], tpss[0]
                r_ap, t_r = rl[cnt[0] % 2]
                cnt[0] += 1
                sl = slice(tt * c.TT, (tt + 1) * c.TT)
                k.op("act", lambda e: e.activation(out=r_ap, in_=ps, func=AF.Relu), reads=[tp], writes=[t_r])
                k.op("dve", lambda e: e.tensor_tensor(out=hbf[:, m, sl], in0=r_ap, in1=r_ap, op=ALU.mult),
                     reads=[t_r], writes=[tg_h[m][tt]])
            self.proj_fm([(w1, 0, c.KC, q * c.D, xbf, tg_xbf)], c.D, evac1)

            def evac2(m, tt, pss, tpss, q=q):
                ps, tp = pss[0], tpss[0]
                sl = slice(tt * c.TT, (tt + 1) * c.TT)
                if q == 0:
                    k.op("dve", lambda e: e.scalar_tensor_tensor(out=x32[:, m, sl], in0=x32[:, m, sl], scalar=c.alpha, in1=ps,
                                                                 op0=ALU.mult, op1=ALU.add), reads=[tp, tg_x[m][tt]], writes=[tg_x[m][tt]])
                else:
                    k.op("dve", lambda e: e.tensor_tensor(out=x32[:, m, sl], in0=x32[:, m, sl], in1=ps, op=ALU.add),
                         reads=[tp, tg_x[m][tt]], writes=[tg_x[m][tt]])
            self.proj_fm([(w2, q * c.D, c.KC, 0, hbf, tg_h)], c.D, evac2)
        self.release(mk)

    def load_small(self, name, shape, q="sp"):
        d = self.din(name, [128] + list(shape))
        ap, t = self.alloc(name, list(shape), F32)
        self.k.dma(q, ap, d, writes=[t])
        return ap, t

    def tail(self, x32, tg_x, xbf, tg_xbf):
        c = self.c
        w1 = self.din("w1", [c.D, c.DFF])
        w2 = self.din("w2", [c.DFF, c.D])
        lnp, t_lnp = self.load_small("lnp", [4, c.KC])
        self.layernorm(x32, tg_x, xbf, tg_xbf, lnp[:, 0, :], lnp[:, 1, :], t_lnp)
        self.mlp(x32, tg_x, xbf, tg_xbf, w1, w2)
        self.layernorm(x32, tg_x, xbf, tg_xbf, lnp[:, 2, :], lnp[:, 3, :], t_lnp)
        xo = self.dout("xT_out", [c.D, c.T])
        for kc in range(c.KC):
            self.k.dma("sp", xo[kc * 128:(kc + 1) * 128, :], x32[:, kc, :], reads=[tg_x[kc][tt] for tt in range(c.NTT)])


def sincos(p, a, t_a, shape, name):
    k = p.k
    t1, t_t1 = p.alloc(name + "_t1", shape, F32)
    r, t_r = p.alloc(name + "_r", shape, F32)
    r2, t_r2 = p.alloc(name + "_r2", shape, F32)
    sn, t_sn = p.alloc(name + "_sin", shape, F32)
    cs, t_cs = p.alloc(name + "_cos", shape, F32)
    k.op("dve", lambda e: e.tensor_scalar(out=t1, in0=a, scalar1=1.0 / TWO_PI, scalar2=MAGIC, op0=ALU.mult, op1=ALU.add),
         reads=[t_a], writes=[t_t1])
    k.op("dve", lambda e: e.tensor_scalar(out=t1, in0=t1, scalar1=MAGIC, scalar2=TWO_PI, op0=ALU.subtract, op1=ALU.mult),
         reads=[t_t1], writes=[t_t1])
    k.op("dve", lambda e: e.tensor_tensor(out=r, in0=a, in1=t1, op=ALU.subtract), reads=[t_a, t_t1], writes=[t_r])
    k.op("dve", lambda e: e.tensor_scalar(out=r, in0=r, scalar1=-math.pi, scalar2=math.pi, op0=ALU.max, op1=ALU.min),
         reads=[t_r], writes=[t_r])
    k.op("act", lambda e: e.activation(out=sn, in_=r, func=AF.Sin), reads=[t_r], writes=[t_sn])
    k.op("dve", lambda e: e.tensor_scalar(out=t1, in0=r, scalar1=math.pi / 2, scalar2=-TWO_PI, op0=ALU.is_gt, op1=ALU.mult),
         reads=[t_r], writes=[t_t1])
    k.op("dve", lambda e: e.scalar_tensor_tensor(out=r2, in0=r, scalar=math.pi / 2, in1=t1, op0=ALU.add, op1=ALU.add),
         reads=[t_r, t_t1], writes=[t_r2])
    k.op("dve", lambda e: e.tensor_scalar(out=r2, in0=r2, scalar1=-math.pi, scalar2=math.pi, op0=ALU.max, op1=ALU.min),
         reads=[t_r2], writes=[t_r2])
    k.op("act", lambda e: e.activation(out=cs, in_=r2, func=AF.Sin), reads=[t_r2], writes=[t_cs])
    return sn, t_sn, cs, t_cs


def tt_op(p, eng, out, t_out, a, t_a, b, t_b, op):
    p.k.op(eng, lambda e: e.tensor_tensor(out=out, in0=a, in1=b, op=op), reads=[t_a, t_b], writes=[t_out])


def s5_cst(cfg):
    L = cfg.L
    cst = np.zeros((128, L + 128 + 128 + 8 + 64 + 1), np.float32)
    cst[:, 0:L] = np.arange(1, L + 1, dtype=np.float32)[None, :]
    perm = np.zeros((128, 128), np.float32)
    for kk in range(128):
        perm[kk, (kk + 64) % 128] = 1.0
    cst[:, L:L + 128] = perm
    cst[:, L + 128:L + 256] = np.eye(128, dtype=np.float32)
    rm = np.zeros((128, 8), np.float32)
    for r in range(128):
        rm[r, r // 16] = 1.0
    cst[:, L + 256:L + 264] = rm
    cst[:, L + 264:L + 328] = np.eye(8, dtype=np.float32).reshape(1, 64)
    cst[0:64, L + 328] = 1.0
    cst[64:128, L + 328] = -1.0
    return cst


def build_s5(cfg, phase):
    c = cfg
    p = Prog(cfg)
    k = p.k
    D, KC, T_, TT, L, G = c.D, c.KC, c.T, c.TT, c.L, c.G
    assert L == TT
    NT = c.NTT
    n1 = KC * 64
    xT = p.din("xT", [D, T_])
    w_in = p.din("w_in", [D, D])
    cst, t_cst = p.load_small("cst", [L + 128 + 128 + 8 + 64 + 1])
    iota1 = cst[:, 0:L]
    perm = cst[:, L:L + 128]
    ident = cst[:, L + 128:L + 256]
    rowmask = cst[:, L + 256:L + 264]
    eye8 = cst[:, L + 264:L + 328]
    sgn = cst[:, L + 328:L + 329]
    xbf, tg_xbf = p.alloc_act("xbf", KC, BF16)
    for kc in range(KC):
        k.dma("pool", xbf[:, kc, :], xT[kc * 128:(kc + 1) * 128, :], writes=tg_xbf[kc])
    mk0 = p.mark()
    s5b, t_s5b = p.load_small("s5b", [3 * G + G * 16 + KC])
    lr2, li2, ldt2 = s5b[:, 0:G], s5b[:, G:2 * G], s5b[:, 2 * G:3 * G]
    ccT = s5b[:, 3 * G:3 * G + G * 16]
    dT = s5b[:, 3 * G + G * 16:3 * G + G * 16 + KC]
    bbT, t_bbT = p.alloc("bbT", [KC, 128], F32)
    bbTs, t_bbTs = p.alloc("bbTs", [KC, 128], F32)
    th2s, t_th2s = p.alloc("th2s", [G], F32)
    rho2, t_rho2 = p.alloc("rho2", [G], F32)
    cc2, t_cc2 = p.alloc("cc2", [G * 16], F32)
    E, t_E = p.alloc("Ecarry", [G], F32)
    Send, t_Send = p.alloc("Send", [G], F32)
    mkp = p.mark()
    s5a, t_s5a = p.load_small("s5a", [4 * n1 + KC])
    lrT, liT, breT, bimT = (s5a[:, i * n1:(i + 1) * n1] for i in range(4))
    ldtT = s5a[:, 4 * n1:4 * n1 + KC]

    def tmp(name, n=n1):
        return p.alloc(name, [n], F32)

    dt1, t_dt1 = tmp("dt1", KC)
    k.op("act", lambda e: e.activation(out=dt1, in_=ldtT, func=AF.Exp), reads=[t_s5a], writes=[t_dt1])
    dt1b = dt1.unsqueeze(2).to_broadcast([128, KC, 64])

    def v3(ap):
        return ap.rearrange("p (a b) -> p a b", a=KC)

    lrd, t_lrd = tmp("lrd")
    lid, t_lid = tmp("lid")
    k.op("dve", lambda e: e.tensor_tensor(out=v3(lrd), in0=v3(lrT), in1=dt1b, op=ALU.mult), reads=[t_s5a, t_dt1], writes=[t_lrd])
    k.op("dve", lambda e: e.tensor_tensor(out=v3(lid), in0=v3(liT), in1=dt1b, op=ALU.mult), reads=[t_s5a, t_dt1], writes=[t_lid])
    mag, t_mag = tmp("mag")
    k.op("act", lambda e: e.activation(out=mag, in_=lrd, func=AF.Exp), reads=[t_lrd], writes=[t_mag])
    sn, t_sn, cs, t_cs = sincos(p, lid, t_lid, [n1], "sc1")
    ar, t_ar = tmp("ar")
    ai, t_ai = tmp("ai")
    tt_op(p, "dve", ar, t_ar, mag, t_mag, cs, t_cs, ALU.mult)
    tt_op(p, "dve", ai, t_ai, mag, t_mag, sn, t_sn, ALU.mult)
    k.op("dve", lambda e: e.tensor_scalar(out=ar, in0=ar, scalar1=-1.0, scalar2=None, op0=ALU.add), reads=[t_ar], writes=[t_ar])
    den, t_den = tmp("den")
    t2_, t_t2 = tmp("t2_")
    tt_op(p, "dve", den, t_den, lrT, t_s5a, lrT, t_s5a, ALU.mult)
    tt_op(p, "dve", t2_, t_t2, liT, t_s5a, liT, t_s5a, ALU.mult)
    tt_op(p, "dve", den, t_den, den, t_den, t2_, t_t2, ALU.add)
    k.op("dve", lambda e: e.reciprocal(out=den, in_=den), reads=[t_den], writes=[t_den])
    zr, t_zr = tmp("zr")
    zi, t_zi = tmp("zi")
    tt_op(p, "dve", zr, t_zr, ar, t_ar, lrT, t_s5a, ALU.mult)
    tt_op(p, "dve", t2_, t_t2, ai, t_ai, liT, t_s5a, ALU.mult)
    tt_op(p, "dve", zr, t_zr, zr, t_zr, t2_, t_t2, ALU.add)
    tt_op(p, "dve", zr, t_zr, zr, t_zr, den, t_den, ALU.mult)
    tt_op(p, "dve", zi, t_zi, ai, t_ai, lrT, t_s5a, ALU.mult)
    tt_op(p, "dve", t2_, t_t2, ar, t_ar, liT, t_s5a, ALU.mult)
    tt_op(p, "dve", zi, t_zi, zi, t_zi, t2_, t_t2, ALU.subtract)
    tt_op(p, "dve", zi, t_zi, zi, t_zi, den, t_den, ALU.mult)
    a1, t_a1 = tmp("a1")
    a2, t_a2 = tmp("a2")
    bbT4 = bbT.rearrange("p k (r q) -> p k r q", r=2)
    bbTs4 = bbTs.rearrange("p k (r q) -> p k r q", r=2)
    tt_op(p, "dve", a1, t_a1, zr, t_zr, breT, t_s5a, ALU.mult)
    tt_op(p, "dve", a2, t_a2, zi, t_zi, bimT, t_s5a, ALU.mult)
    k.op("dve", lambda e: e.tensor_tensor(out=bbT4[:, :, 0, :], in0=v3(a1), in1=v3(a2), op=ALU.subtract), reads=[t_a1, t_a2], writes=[t_bbT])
    k.op("dve", lambda e: e.tensor_tensor(out=bbTs4[:, :, 1, :], in0=v3(a1), in1=v3(a2), op=ALU.subtract), reads=[t_a1, t_a2], writes=[t_bbTs])
    tt_op(p, "dve", a1, t_a1, zr, t_zr, bimT, t_s5a, ALU.mult)
    tt_op(p, "dve", a2, t_a2, zi, t_zi, breT, t_s5a, ALU.mult)
    k.op("dve", lambda e: e.tensor_tensor(out=bbT4[:, :, 1, :], in0=v3(a1), in1=v3(a2), op=ALU.add), reads=[t_a1, t_a2], writes=[t_bbT])
    k.op("dve", lambda e: e.tensor_tensor(out=bbTs4[:, :, 0, :], in0=v3(a1), in1=v3(a2), op=ALU.add), reads=[t_a1, t_a2], writes=[t_bbTs])
    dt2, t_dt2 = tmp("dt2", G)
    k.op("act", lambda e: e.activation(out=dt2, in_=ldt2, func=AF.Exp), reads=[t_s5b], writes=[t_dt2])
    th2, t_th2 = tmp("th2", G)
    tt_op(p, "dve", th2, t_th2, li2, t_s5b, dt2, t_dt2, ALU.mult)
    k.op("dve", lambda e: e.tensor_scalar(out=th2s, in0=th2, scalar1=sgn, scalar2=None, op0=ALU.mult), reads=[t_th2, t_cst], writes=[t_th2s])
    lrd2, t_lrd2 = tmp("lrd2", G)
    tt_op(p, "dve", lrd2, t_lrd2, lr2, t_s5b, dt2, t_dt2, ALU.mult)
    k.op("act", lambda e: e.activation(out=rho2, in_=lrd2, func=AF.Exp), reads=[t_lrd2], writes=[t_rho2])
    k.op("dve", lambda e: e.tensor_scalar(out=cc2, in0=ccT, scalar1=sgn, scalar2=None, op0=ALU.mult), reads=[t_s5b, t_cst], writes=[t_cc2])
    if phase == "B":
        sprev = p.din("sprev", [128, 6, G])
        sp_sb, t_sp = p.alloc("sprev", [6, G], F32)
        k.dma("sp", sp_sb, sprev, writes=[t_sp])
        pm, t_pm = tmp("pm", G)
        k.op("act", lambda e: e.activation(out=pm, in_=lrd2, func=AF.Exp, scale=float(T_)), reads=[t_lrd2], writes=[t_pm])
        ang, t_ang = tmp("ang", G)
        k.op("dve", lambda e: e.tensor_scalar(out=ang, in0=th2s, scalar1=float(T_), scalar2=None, op0=ALU.mult), reads=[t_th2s], writes=[t_ang])
        snA, t_snA, csA, t_csA = sincos(p, ang, t_ang, [G], "scA")
        pre, t_pre = tmp("pre", G)
        pim, t_pim = tmp("pim", G)
        tt_op(p, "dve", pre, t_pre, pm, t_pm, csA, t_csA, ALU.mult)
        tt_op(p, "dve", pim, t_pim, pm, t_pm, snA, t_snA, ALU.mult)
        k.op("dve", lambda e: e.tensor_scalar(out=pim, in0=pim, scalar1=-1.0, scalar2=None, op0=ALU.mult), reads=[t_pim], writes=[t_pim])
        Es, t_Es = tmp("Es", G)
        e1, t_e1 = tmp("e1", G)
        e2, t_e2 = tmp("e2", G)
        e3, t_e3 = tmp("e3", G)
        k.op("dve", lambda e: e.tensor_copy(out=E, in_=sp_sb[:, 0, :]), reads=[t_sp], writes=[t_E])
        k.op("dve", lambda e: e.tensor_copy(out=Es, in_=sp_sb[:, 3, :]), reads=[t_sp], writes=[t_Es])
        for j in (1, 2):
            tt_op(p, "dve", e1, t_e1, pre, t_pre, E, t_E, ALU.mult)
            tt_op(p, "dve", e2, t_e2, pim, t_pim, Es, t_Es, ALU.mult)
            tt_op(p, "dve", e3, t_e3, pre, t_pre, Es, t_Es, ALU.mult)
            tt_op(p, "dve", Es, t_Es, pim, t_pim, E, t_E, ALU.mult)
            tt_op(p, "dve", E, t_E, e1, t_e1, e2, t_e2, ALU.add)
            tt_op(p, "dve", Es, t_Es, e3, t_e3, Es, t_Es, ALU.subtract)
            k.op("dve", lambda e, j=j: e.tensor_tensor(out=E, in0=E, in1=sp_sb[:, j, :], op=ALU.add), reads=[t_E, t_sp], writes=[t_E])
            k.op("dve", lambda e, j=j: e.tensor_tensor(out=Es, in0=Es, in1=sp_sb[:, 3 + j, :], op=ALU.add), reads=[t_Es, t_sp], writes=[t_Es])
    else:
        k.op("dve", lambda e: e.memset(E, 0.0), writes=[t_E])
    p.release(mkp)
    ubf, tg_u = p.alloc_act("ubf", KC, BF16)
    def evac_u(m, tt, pss, tpss):
        sl = slice(tt * TT, (tt + 1) * TT)
        k.op("act", lambda e: e.activation(out=ubf[:, m, sl], in_=pss[0], func=AF.Copy), reads=[tpss[0]], writes=[tg_u[m][tt]])
    p.proj_fm([(w_in, 0, KC, 0, xbf, tg_xbf)], D, evac_u)
    BB = [p.alloc(f"BB{i}", [8, 128], BF16) for i in range(2)]
    BBs = [p.alloc(f"BBs{i}", [8, 128], BF16) for i in range(2)]
    Cm = [p.alloc(f"Cm{i}", [8, 128], BF16) for i in range(2)]
    Dm = [p.alloc(f"Dm{i}", [128], BF16) for i in range(2)]
    cosT = [p.alloc(f"cosT{i}", [L], F32) for i in range(2)]
    sinT = [p.alloc(f"sinT{i}", [L], F32) for i in range(2)]
    ga = [p.alloc(f"ga{i}", [L], F32) for i in range(2)]
    gt = [p.alloc(f"gt{i}", [L], F32) for i in range(2)]
    gr = [p.alloc(f"gr{i}", [L], F32) for i in range(2)]
    gr2 = [p.alloc(f"gr2{i}", [L], F32) for i in range(2)]
    t1 = [p.alloc(f"t1{i}", [L], F32) for i in range(2)]
    t2 = [p.alloc(f"t2{i}", [L], F32) for i in range(2)]
    zz = [p.alloc(f"zz{i}", [L], F32) for i in range(2)]
    t3 = [p.alloc(f"t3{i}", [L], F32) for i in range(2)]
    s32 = [p.alloc(f"s32{i}", [L], F32) for i in range(2)]
    sbf = [p.alloc(f"sbf{i}", [L], BF16) for i in range(2)]
    gq = [p.alloc(f"gq{i}", [L], F32) for i in range(2)]
    gi = [p.alloc(f"gi{i}", [L], F32) for i in range(2)]
    full = phase == "B"
    hbf, tg_h = xbf, tg_xbf
    cnt = 0
    C1 = 0.7978845608028654
    C2 = C1 * 0.044715
    for kc in range(KC):
        bb_ap, t_bb = BB[kc % 2]
        bbs_ap, t_bbs = BBs[kc % 2]
        cm_ap, t_cm = Cm[kc % 2]
        dm_ap, t_dm = Dm[kc % 2]
        rmb = rowmask.unsqueeze(2).to_broadcast([128, 8, 128])
        k.op("dve", lambda e, bb_ap=bb_ap, kc=kc: e.tensor_tensor(out=bb_ap, in0=bbT[:, kc, :].unsqueeze(1).to_broadcast([128, 8, 128]),
                                                                 in1=rmb, op=ALU.mult), reads=[t_bbT, t_cst], writes=[t_bb])
        k.op("dve", lambda e, bbs_ap=bbs_ap, kc=kc: e.tensor_tensor(out=bbs_ap, in0=bbTs[:, kc, :].unsqueeze(1).to_broadcast([128, 8, 128]),
                                                                   in1=rmb, op=ALU.mult), reads=[t_bbTs, t_cst], writes=[t_bbs])
        if full:
            cc_k = cc2[:, kc * 128:(kc + 1) * 128].rearrange("p (a b) -> p a b", a=8)
            k.op("dve", lambda e, cm_ap=cm_ap, cc_k=cc_k: e.tensor_tensor(
                out=cm_ap.rearrange("p a (b c) -> p a b c", b=8), in0=cc_k.unsqueeze(2).to_broadcast([128, 8, 8, 16]),
                in1=eye8.rearrange("p (a b) -> p a b", a=8).unsqueeze(3).to_broadcast([128, 8, 8, 16]), op=ALU.mult),
                reads=[t_cc2, t_cst], writes=[t_cm])
            k.op("dve", lambda e, dm_ap=dm_ap, kc=kc: e.tensor_scalar(out=dm_ap, in0=ident, scalar1=dT[:, kc:kc + 1], scalar2=None, op0=ALU.mult),
                 reads=[t_cst, t_s5b], writes=[t_dm])
            for tile in range(NT):
                sl = slice(tile * L, (tile + 1) * L)
                k.op("pe", lambda e, tile=tile, sl=sl, dm_ap=dm_ap, kc=kc: e.matmul(p.ps[3 + tile][:, 0:L], lhsT=dm_ap, rhs=ubf[:, kc, sl],
                                                                                start=True, stop=False),
                     reads=[t_dm, tg_u[kc][tile]], writes=[p.tps[3 + tile]])
        for g8 in range(8):
            g = kc * 8 + g8
            gb = g % 2
            ca, t_ca = cosT[gb]
            sa, t_sa = sinT[gb]
            a_, t_a_ = ga[gb]
            tq, t_tq = gt[gb]
            r_, t_r_ = gr[gb]
            r2_, t_r2_ = gr2[gb]
            thg = th2s[:, g:g + 1]
            k.op("pool", lambda e, a_=a_, thg=thg: e.tensor_scalar(out=a_, in0=iota1, scalar1=thg, scalar2=None, op0=ALU.mult),
                 reads=[t_cst, t_th2s], writes=[t_a_])
            k.op("pool", lambda e, a_=a_, tq=tq: e.tensor_scalar(out=tq, in0=a_, scalar1=1.0 / TWO_PI, scalar2=MAGIC, op0=ALU.mult, op1=ALU.add),
                 reads=[t_a_], writes=[t_tq])
            k.op("pool", lambda e, tq=tq: e.tensor_scalar(out=tq, in0=tq, scalar1=-MAGIC, scalar2=TWO_PI, op0=ALU.add, op1=ALU.mult),
                 reads=[t_tq], writes=[t_tq])
            k.op("pool", lambda e, a_=a_, tq=tq, r_=r_: e.tensor_tensor(out=r_, in0=a_, in1=tq, op=ALU.subtract),
                 reads=[t_a_, t_tq], writes=[t_r_])
            k.op("pool", lambda e, r_=r_: e.tensor_scalar(out=r_, in0=r_, scalar1=math.pi, scalar2=-math.pi, op0=ALU.min, op1=ALU.max),
                 reads=[t_r_], writes=[t_r_])
            k.op("act", lambda e, sa=sa, r_=r_: e.activation(out=sa, in_=r_, func=AF.Sin), reads=[t_r_], writes=[t_sa])
            k.op("dve", lambda e, tq=tq, r_=r_: e.tensor_scalar(out=tq, in0=r_, scalar1=math.pi / 2, scalar2=-TWO_PI, op0=ALU.is_gt, op1=ALU.mult),
                 reads=[t_r_], writes=[t_tq])
            k.op("dve", lambda e, r2_=r2_, r_=r_, tq=tq: e.scalar_tensor_tensor(out=r2_, in0=r_, scalar=math.pi / 2, in1=tq, op0=ALU.add, op1=ALU.add),
                 reads=[t_r_, t_tq], writes=[t_r2_])
            k.op("act", lambda e, ca=ca, r2_=r2_: e.activation(out=ca, in_=r2_, func=AF.Sin), reads=[t_r2_], writes=[t_ca])
            rho_b = rho2[:, g:g + 1].to_broadcast([128, L])
            prev_s = None
            for tile in range(NT):
                sl = slice(tile * L, (tile + 1) * L)
                b = cnt % 2
                cnt += 1
                t1a, t_t1a = t1[b]
                t2a, t_t2a = t2[b]
                za, t_za = zz[b]
                t3a, t_t3a = t3[b]
                sa32, t_s32 = s32[b]
                sbfa, t_sbf = sbf[b]
                psX, psXs, psZ = p.ps[0][:, 0:L], p.ps[1][:, 0:L], p.ps[2][:, 0:L]
                k.op("pe", lambda e, psX=psX, bb_ap=bb_ap, g8=g8, kc=kc, sl=sl: e.matmul(psX, lhsT=bb_ap[:, g8, :], rhs=ubf[:, kc, sl], start=True, stop=True),
                     reads=[t_bb, tg_u[kc][tile]], writes=[p.tps[0]])
                k.op("pe", lambda e, psXs=psXs, bbs_ap=bbs_ap, g8=g8, kc=kc, sl=sl: e.matmul(psXs, lhsT=bbs_ap[:, g8, :], rhs=ubf[:, kc, sl], start=True, stop=True),
                     reads=[t_bbs, tg_u[kc][tile]], writes=[p.tps[1]])
                k.op("dve", lambda e, t1a=t1a, ca=ca, psX=psX: e.tensor_tensor(out=t1a, in0=ca, in1=psX, op=ALU.mult),
                     reads=[t_ca, p.tps[0]], writes=[t_t1a])
                k.op("dve", lambda e, t2a=t2a, sa=sa, psXs=psXs: e.tensor_tensor(out=t2a, in0=sa, in1=psXs, op=ALU.mult),
                     reads=[t_sa, p.tps[1]], writes=[t_t2a])
                k.op("dve", lambda e, t1a=t1a, t2a=t2a: e.tensor_tensor(out=t1a, in0=t1a, in1=t2a, op=ALU.add),
                     reads=[t_t1a, t_t2a], writes=[t_t1a])
                if tile == 0:
                    init, t_init = E[:, g:g + 1], t_E
                else:
                    init, t_init = prev_s
                k.op("dve", lambda e, za=za, rho_b=rho_b, t1a=t1a, init=init: e.tensor_tensor_scan(out=za, data0=rho_b, data1=t1a, initial=init,
                                                                                              op0=ALU.mult, op1=ALU.add),
                     reads=[t_rho2, t_t1a, t_init], writes=[t_za])
                last = tile == NT - 1
                if full:
                    k.op("pe", lambda e, psZ=psZ, za=za: e.matmul(psZ, lhsT=perm, rhs=za, start=True, stop=True),
                         reads=[t_cst, t_za], writes=[p.tps[2]])
                    k.op("dve", lambda e, t3a=t3a, ca=ca, za=za: e.tensor_tensor(out=t3a, in0=ca, in1=za, op=ALU.mult),
                         reads=[t_ca, t_za], writes=[t_t3a])
                    k.op("dve", lambda e, t2a=t2a, sa=sa, psZ=psZ: e.tensor_tensor(out=t2a, in0=sa, in1=psZ, op=ALU.mult),
                         reads=[t_sa, p.tps[2]], writes=[t_t2a])
                    k.op("dve", lambda e, sa32=sa32, t3a=t3a, t2a=t2a: e.tensor_tensor(out=sa32, in0=t3a, in1=t2a, op=ALU.subtract),
                         reads=[t_t3a, t_t2a], writes=[t_s32])
                    k.op("act", lambda e, sbfa=sbfa, sa32=sa32: e.activation(out=sbfa, in_=sa32, func=AF.Copy), reads=[t_s32], writes=[t_sbf])
                    k.op("pe", lambda e, tile=tile, cm_ap=cm_ap, g8=g8, sbfa=sbfa: e.matmul(p.ps[3 + tile][:, 0:L], lhsT=cm_ap[:, g8, :], rhs=sbfa,
                                                                                       start=False, stop=(g8 == 7)),
                         reads=[t_cm, t_sbf], writes=[p.tps[3 + tile]])
                    prev_s = (sa32[:, L - 1:L], t_s32)
                else:
                    k.op("pe", lambda e, za=za: e.matmul(p.ps[2][:, 0:2], lhsT=perm, rhs=za[:, L - 2:L], start=True, stop=True),
                         reads=[t_cst, t_za], writes=[p.tps[2]])
                    k.op("dve", lambda e, t3a=t3a, ca=ca, za=za: e.tensor_tensor(out=t3a[:, 0:2], in0=ca[:, L - 2:L], in1=za[:, L - 2:L], op=ALU.mult),
                         reads=[t_ca, t_za], writes=[t_t3a])
                    k.op("dve", lambda e, t2a=t2a, sa=sa: e.tensor_tensor(out=t2a[:, 0:2], in0=sa[:, L - 2:L], in1=p.ps[2][:, 0:2], op=ALU.mult),
                         reads=[t_sa, p.tps[2]], writes=[t_t2a])
                    k.op("dve", lambda e, sa32=sa32, t3a=t3a, t2a=t2a: e.tensor_tensor(out=sa32[:, L - 2:L], in0=t3a[:, 0:2], in1=t2a[:, 0:2], op=ALU.subtract),
                         reads=[t_t3a, t_t2a], writes=[t_s32])
                    prev_s = (sa32[:, L - 1:L], t_s32)
                if last and not full:
                    k.op("act", lambda e, g=g, sa32=sa32: e.activation(out=Send[:, g:g + 1], in_=sa32[:, L - 1:L], func=AF.Copy),
                         reads=[t_s32], writes=[t_Send])
        if full:
            for tile in range(NT):
                sl = slice(tile * L, (tile + 1) * L)
                yps, t_yps = p.ps[3 + tile][:, 0:L], p.tps[3 + tile]
                q_ap, t_q = gq[tile % 2]
                i_ap, t_i = gi[tile % 2]
                k.op("act", lambda e, q_ap=q_ap, yps=yps: e.activation(out=q_ap, in_=yps, func=AF.Square), reads=[t_yps], writes=[t_q])
                k.op("dve", lambda e, q_ap=q_ap: e.tensor_scalar(out=q_ap, in0=q_ap, scalar1=C2, scalar2=C1, op0=ALU.mult, op1=ALU.add),
                     reads=[t_q], writes=[t_q])
                k.op("dve", lambda e, i_ap=i_ap, q_ap=q_ap, yps=yps: e.tensor_tensor(out=i_ap, in0=q_ap, in1=yps, op=ALU.mult),
                     reads=[t_q, t_yps], writes=[t_i])
                k.op("act", lambda e, i_ap=i_ap: e.activation(out=i_ap, in_=i_ap, func=AF.Sigmoid, scale=2.0), reads=[t_i], writes=[t_i])
                k.op("dve", lambda e, i_ap=i_ap, yps=yps, kc=kc, sl=sl: e.tensor_tensor(out=hbf[:, kc, sl], in0=i_ap, in1=yps, op=ALU.mult),
                     reads=[t_i, t_yps], writes=[tg_h[kc][tile]])
    if not full:
        so = p.dout("s_end", [128, G])
        k.dma("sp", so, Send, reads=[t_Send])
        return p
    p.release(mk0)
    w_out = p.din("w_out", [D, D])
    w_gate = p.din("w_gate", [D, D])
    x32, tg_x = p.alloc_act("x32", KC, F32)
    for kc in range(KC):
        k.dma("sp", x32[:, kc, :], xT[kc * 128:(kc + 1) * 128, :], writes=tg_x[kc])
    mk1 = p.mark()
    sg = [p.alloc(f"sg{i}", [TT], F32) for i in range(2)]
    ff = [p.alloc(f"ff{i}", [TT], F32) for i in range(2)]
    cn = [0]

    def evac_f(m, tt, pss, tpss):
        sl = slice(tt * TT, (tt + 1) * TT)
        s_ap, t_s = sg[cn[0] % 2]
        f_ap, t_f = ff[cn[0] % 2]
        cn[0] += 1
        k.op("act", lambda e: e.activation(out=s_ap, in_=pss[1], func=AF.Sigmoid), reads=[tpss[1]], writes=[t_s])
        k.op("dve", lambda e: e.tensor_tensor(out=f_ap, in0=s_ap, in1=pss[0], op=ALU.mult), reads=[t_s, tpss[0]], writes=[t_f])
        k.op("dve", lambda e: e.scalar_tensor_tensor(out=x32[:, m, sl], in0=x32[:, m, sl], scalar=c.alpha, in1=f_ap, op0=ALU.mult, op1=ALU.add),
             reads=[t_f, tg_x[m][tt]], writes=[tg_x[m][tt]])
    p.proj_fm([(w_out, 0, KC, 0, hbf, tg_h), (w_gate, 0, KC, 0, hbf, tg_h)], D, evac_f)
    p.release(mk1)
    p.tail(x32, tg_x, xbf, tg_xbf)
    return p


def prep_s5(cfg, lam_re, lam_im, log_dt, b_re, b_im, c_re, c_im, d):
    KC, G = cfg.KC, cfg.G
    n1 = KC * 64

    def rep1(a):
        a = a.reshape(KC, 8, 64).transpose(1, 0, 2)
        a = np.repeat(a[:, None], 16, axis=1)
        return a.reshape(128, n1)

    def bT(b):
        return b.reshape(KC, 8, 64, 16).transpose(1, 3, 0, 2).reshape(128, n1)

    ldtT = np.repeat(log_dt.reshape(KC, 8).T[:, None, :], 16, axis=1).reshape(128, KC)
    s5a = np.concatenate([rep1(lam_re), rep1(lam_im), bT(b_re), bT(b_im), ldtT], axis=1).astype(np.float32)

    def rep2(a):
        return np.concatenate([a.T, a.T], axis=0)

    ldt2 = np.broadcast_to(log_dt[None, :], (128, G))
    ccT = np.concatenate([c_re.transpose(2, 0, 1), c_im.transpose(2, 0, 1)], axis=0).reshape(128, G * 16)
    dT = d.reshape(KC, 128).T
    s5b = np.concatenate([rep2(lam_re), rep2(lam_im), ldt2, ccT, dT], axis=1).astype(np.float32)
    return np.ascontiguousarray(s5a), np.ascontiguousarray(s5b)


def prep_lnp(cfg, g1, b1, g2, b2):
    return np.ascontiguousarray(np.stack([v.reshape(cfg.KC, 128).T for v in (g1, b1, g2, b2)], axis=1).astype(np.float32))


NEG = -30000.0


def ssd_cst():
    cst = np.zeros((128, 4 * 128), np.float32)
    s = np.arange(128)
    cst[:, 0:128] = (s[:, None] <= s[None, :]).astype(np.float32)
    cst[:, 128:256] = np.where(s[None, :] >= s[:, None], 0.0, NEG)
    cst[:, 256:384] = np.eye(128, dtype=np.float32)
    cst[:, 384:512] = -cst[:, 0:128]
    return cst


def build_ssd(cfg, phase, NG=8):
    c = cfg
    p = Prog(cfg)
    k = p.k
    D, KC, T_, TT = c.D, c.KC, c.T, c.TT
    NT = c.NTT
    Q = 128
    NCH = T_ // Q
    DI = 2 * D
    H = DI // 64
    assert H == NG * 8
    IN_DIM = 2 * DI + 2 * NG * 128 + H
    full = phase == "B"
    xT = p.din("xT", [D, T_])
    xpT = p.din("xprevT", [D, 4])
    w_in = p.din("w_in", [D, IN_DIM])
    xbf, tg_xbf = p.alloc_act("xbf", KC, BF16)
    for kc in range(KC):
        k.dma("pool", xbf[:, kc, :], xT[kc * 128:(kc + 1) * 128, :], writes=tg_xbf[kc])
    mk0 = p.mark()
    cst, t_cst = p.load_small("cst", [512])
    tri, negmask, ident, ntri = cst[:, 0:128], cst[:, 128:256], cst[:, 256:384], cst[:, 384:512]
    identb, t_identb = p.alloc("identb", [128], BF16)
    k.op("dve", lambda e: e.tensor_copy(out=identb, in_=ident), reads=[t_cst], writes=[t_identb])
    one_c, t_one = p.alloc("one_c", [1], F32)
    k.op("pool", lambda e: e.memset(one_c, 1.0), writes=[t_one])
    hpd = p.din("hp", [128, 3 * H + DI])
    hp, t_hp = p.alloc("hp", [3 * H], F32)
    k.dma("sp", hp, hpd[:, 0:3 * H], writes=[t_hp])
    dtb_b, alog_b, dsk_b = hp[:, 0:H], hp[:, H:2 * H], hp[:, 2 * H:3 * H]
    nwg = [p.alloc(f"nwg{i}", [512], F32) for i in range(2)]
    NXC = (DI + 2 * NG * 128) // 128
    convp, t_convp = p.load_small("convp", [NXC, 5])
    xpb, t_xpb = p.alloc("xpb", [KC, 4], BF16)
    k.dma("pool", xpb, xpT.rearrange("(kc p) t -> p kc t", p=128), writes=[t_xpb])
    if full:
        scr = p.dout("scr_yn", [DI, T_], BF16)
        t_scr = [T(f"scr{g}") for g in range(NG)]
        ystg = [p.alloc(f"ystg{i}", [4, 128], BF16) for i in range(2)]
    dt_all, t_dt = p.alloc("dt_all", [NCH, H], F32)
    adt_all, t_adt = p.alloc("adt_all", [NCH, H], F32)
    acs_all, t_acs = p.alloc("acs_all", [NCH, H], F32)
    dd_all, t_dd = p.alloc("dd_all", [NCH, H], F32)
    eacs_all, t_eacs = p.alloc("eacs_all", [NCH, H], F32)
    etot_all, t_etot = p.alloc("etot_all", [NCH, H], F32)
    TOT, t_TOT = p.alloc("TOT", [H], F32)
    aneg, t_aneg = p.alloc("aneg", [H], F32)
    k.op("act", lambda e: e.activation(out=aneg, in_=alog_b, func=AF.Exp), reads=[t_hp], writes=[t_aneg])
    k.op("dve", lambda e: e.tensor_scalar(out=aneg, in0=aneg, scalar1=-1.0, scalar2=None, op0=ALU.mult), reads=[t_aneg], writes=[t_aneg])
    wdt, t_wdt = p.alloc("wdt", [KC, H], BF16)
    k.dma("pool", wdt, w_in[:, 2 * DI + 2 * NG * 128:IN_DIM].rearrange("(kc p) n -> p kc n", p=128), writes=[t_wdt])
    v_, t_v = p.alloc("dt_v", [H], F32)
    for ch in range(NCH):
        tsl = slice(ch * Q, (ch + 1) * Q)
        tt = (ch * Q) // TT
        ps = p.ps[0][:, 0:H]
        for kc in range(KC):
            k.op("pe", lambda e, ps=ps, kc=kc, tsl=tsl: e.matmul(ps, lhsT=xbf[:, kc, tsl], rhs=wdt[:, kc, :], start=(kc == 0), stop=(kc == KC - 1)),
                 reads=[tg_xbf[kc][tt], t_wdt], writes=[p.tps[0]])
        k.op("dve", lambda e, ps=ps: e.tensor_tensor(out=v_, in0=ps, in1=dtb_b, op=ALU.add), reads=[p.tps[0], t_hp], writes=[t_v])
        k.op("act", lambda e: e.activation(out=v_, in_=v_, func=AF.Exp), reads=[t_v], writes=[t_v])
        k.op("act", lambda e, ch=ch: e.activation(out=dt_all[:, ch, :], in_=v_, func=AF.Ln, bias=one_c), reads=[t_v, t_one], writes=[t_dt])
        k.op("dve", lambda e, ch=ch: e.tensor_tensor(out=adt_all[:, ch, :], in0=dt_all[:, ch, :], in1=aneg, op=ALU.mult), reads=[t_dt, t_aneg], writes=[t_adt])
        ps1, ps2 = p.ps[1][:, 0:H], p.ps[2][:, 0:H]
        k.op("pe", lambda e, ps1=ps1, ch=ch: e.matmul(ps1, lhsT=tri, rhs=adt_all[:, ch, :], start=True, stop=True), reads=[t_cst, t_adt], writes=[p.tps[1]])
        k.op("pe", lambda e, ps2=ps2, ch=ch: e.matmul(ps2, lhsT=p.ones128, rhs=adt_all[:, ch, :], start=True, stop=True), reads=[p.t_ones128, t_adt], writes=[p.tps[2]])
        k.op("act", lambda e, ps1=ps1, ch=ch: e.activation(out=acs_all[:, ch, :], in_=ps1, func=AF.Copy), reads=[p.tps[1]], writes=[t_acs])
        k.op("act", lambda e, ps1=ps1, ch=ch: e.activation(out=eacs_all[:, ch, :], in_=ps1, func=AF.Exp), reads=[p.tps[1]], writes=[t_eacs])
        k.op("act", lambda e, ps2=ps2, ch=ch: e.activation(out=etot_all[:, ch, :], in_=ps2, func=AF.Exp), reads=[p.tps[2]], writes=[t_etot])
        k.op("dve", lambda e, ps2=ps2, ch=ch: e.tensor_tensor(out=dd_all[:, ch, :], in0=ps2, in1=acs_all[:, ch, :], op=ALU.subtract), reads=[p.tps[2], t_acs], writes=[t_dd])
        k.op("act", lambda e, ch=ch: e.activation(out=dd_all[:, ch, :], in_=dd_all[:, ch, :], func=AF.Exp), reads=[t_dd], writes=[t_dd])
        k.op("dve", lambda e, ch=ch: e.tensor_tensor(out=dd_all[:, ch, :], in0=dd_all[:, ch, :], in1=dt_all[:, ch, :], op=ALU.mult), reads=[t_dd, t_dt], writes=[t_dd])
        if ch == 0:
            k.op("dve", lambda e, ps2=ps2: e.tensor_copy(out=TOT, in_=ps2), reads=[p.tps[2]], writes=[t_TOT])
        else:
            k.op("dve", lambda e, ps2=ps2: e.tensor_tensor(out=TOT, in0=TOT, in1=ps2, op=ALU.add), reads=[p.tps[2], t_TOT], writes=[t_TOT])
    raw = [p.alloc(f"raw{i}", [4 + T_], F32) for i in range(2)]
    acc = [p.alloc(f"acc{i}", [T_], F32) for i in range(2)]
    xs_fm, t_xsfm = p.alloc("xs_fm", [4, T_], BF16)
    BT, t_BT = p.alloc("BT", [T_], BF16)
    CT, t_CT = p.alloc("CT", [T_], BF16)
    xs_tm, t_xstm = p.alloc("xs_tm", [NCH, 512], BF16)
    B_tm, t_Btm = p.alloc("B_tm", [NCH, 128], BF16)
    ent32, t_ent32 = p.alloc("ent32", [512], F32)
    entbf, t_entbf = p.alloc("entbf", [512], BF16)
    if full:
        adtri = [p.alloc(f"adtri{i}", [8, 128], F32) for i in range(2)]
        arg = [p.alloc(f"arg{i}", [8, 128], F32) for i in range(2)]
        Mt = [p.alloc(f"Mt{i}", [8, 128], BF16) for i in range(2)]
        xdt = [p.alloc(f"xdt{i}", [512], BF16) for i in range(2)]
        xsd = [p.alloc(f"xsd{i}", [512], BF16) for i in range(2)]
        y32 = [p.alloc(f"y32{i}", [512], F32) for i in range(2)]
        ytmp = [p.alloc(f"ytmp{i}", [512], F32) for i in range(2)]
        szb = [p.alloc(f"szb{i}", [512], F32) for i in range(2)]
        ynb = [p.alloc(f"ynb{i}", [512], BF16) for i in range(2)]
        ss = [p.alloc(f"ss{i}", [1], F32) for i in range(2)]
        sprev = p.din("sprev", [3, NG, 128, 512])
        totprev = p.din("totprev", [128, 3, H])
        tp_sb, t_tp = p.alloc("totprev", [3, H], F32)
        k.dma("sp", tp_sb, totprev, writes=[t_tp])
        k.op("act", lambda e: e.activation(out=tp_sb, in_=tp_sb, func=AF.Exp), reads=[t_tp], writes=[t_tp])
        sp_sb, t_spsb = p.alloc("sp_sb", [512], F32)
    xdd = [p.alloc(f"xdd{i}", [512], BF16) for i in range(2)]
    if not full:
        s_out = p.dout("s_end", [NG, 128, 512])
        tot_out = p.dout("tot", [128, H])
        k.dma("sp", tot_out, TOT, reads=[t_TOT])
    cnt = 0
    for g in range(NG):
        hs = slice(8 * g, 8 * g + 8)
        col_xs = DI + g * 512
        col_B = 2 * DI + g * 128
        col_C = 2 * DI + NG * 128 + g * 128
        wv_xs, tw_xs = p.wload(w_in, 0, KC, col_xs, 512)
        wv_bc, tw_bc = None, None
        chunk_specs = [(wv_xs, tw_xs, j * 128, (g * 512) // 128 + j) for j in range(4)]
        for ci in range(6):
            if ci == 4:
                apw, tw = p.wbuf[p.wq % 2]
                p.wq += 1
                vw = apw[:, 0:KC * 256].rearrange("p (a b) -> p a b", a=KC)
                k.dma("pool", vw[:, :, 0:128], w_in[:, col_B:col_B + 128].rearrange("(kc p) n -> p kc n", p=128), writes=[tw])
                k.dma("pool", vw[:, :, 128:256], w_in[:, col_C:col_C + 128].rearrange("(kc p) n -> p kc n", p=128), writes=[tw])
                wv_bc, tw_bc = vw, tw
            if ci < 4:
                wv, tw, co, cch = chunk_specs[ci]
            elif ci == 4:
                wv, tw, co, cch = wv_bc, tw_bc, 0, DI // 128 + g
            else:
                wv, tw, co, cch = wv_bc, tw_bc, 128, DI // 128 + NG + g
            r_ap, t_r = raw[ci % 2]
            a_ap, t_a = acc[ci % 2]
            for tt in range(NT):
                b = (cnt % 2)
                cnt += 1
                ps = p.ps[b][:, 0:TT]
                for kc in range(KC):
                    k.op("pe", lambda e, ps=ps, wv=wv, kc=kc, co=co, tt=tt: e.matmul(ps, lhsT=wv[:, kc, co:co + 128], rhs=xbf[:, kc, tt * TT:(tt + 1) * TT],
                                                                               start=(kc == 0), stop=(kc == KC - 1)),
                         reads=[tw, tg_xbf[kc][tt]], writes=[p.tps[b]])
                k.op("act", lambda e, ps=ps, r_ap=r_ap, tt=tt: e.activation(out=r_ap[:, 4 + tt * TT:4 + (tt + 1) * TT], in_=ps, func=AF.Copy),
                     reads=[p.tps[b]], writes=[t_r])
            psh = p.ps[2][:, 0:4]
            for kc in range(KC):
                k.op("pe", lambda e, psh=psh, wv=wv, kc=kc, co=co: e.matmul(psh, lhsT=wv[:, kc, co:co + 128], rhs=xpb[:, kc, :], start=(kc == 0), stop=(kc == KC - 1)),
                     reads=[tw, t_xpb], writes=[p.tps[2]])
            k.op("act", lambda e, psh=psh, r_ap=r_ap: e.activation(out=r_ap[:, 0:4], in_=psh, func=AF.Copy), reads=[p.tps[2]], writes=[t_r])
            k.op("dve", lambda e, a_ap=a_ap, r_ap=r_ap, cch=cch: e.tensor_scalar(out=a_ap, in0=r_ap[:, 1:1 + T_], scalar1=convp[:, cch, 0:1], scalar2=None, op0=ALU.mult),
                 reads=[t_r, t_convp], writes=[t_a])
            for kk in (1, 2, 3):
                k.op("dve", lambda e, a_ap=a_ap, r_ap=r_ap, cch=cch, kk=kk: e.scalar_tensor_tensor(out=a_ap, in0=r_ap[:, 1 + kk:1 + kk + T_], scalar=convp[:, cch, kk:kk + 1],
                                                                                             in1=a_ap, op0=ALU.mult, op1=ALU.add),
                     reads=[t_r, t_convp, t_a], writes=[t_a])
            if ci < 4:
                dst, t_dst = xs_fm[:, ci, :], t_xsfm
            elif ci == 4:
                dst, t_dst = BT, t_BT
            else:
                dst, t_dst = CT, t_CT
            k.op("act", lambda e, dst=dst, a_ap=a_ap, cch=cch: e.activation(out=dst, in_=a_ap, func=AF.Silu, bias=convp[:, cch, 4:5]),
                 reads=[t_a, t_convp], writes=[t_dst])
        psT = p.ps[7].bitcast(BF16)
        for ch in range(NCH):
            tsl = slice(ch * Q, (ch + 1) * Q)
            for j in range(5):
                src = xs_fm[:, j, tsl] if j < 4 else BT[:, tsl]
                t_src = t_xsfm if j < 4 else t_BT
                k.op("pe", lambda e, src=src: e.transpose(out=psT[:, 0:128], in_=src, identity=identb), reads=[t_src, t_identb], writes=[p.tps[7]])
                if j < 4:
                    k.op("act", lambda e, ch=ch, j=j: e.activation(out=xs_tm[:, ch, j * 128:(j + 1) * 128], in_=psT[:, 0:128], func=AF.Copy),
                         reads=[p.tps[7]], writes=[t_xstm])
                else:
                    k.op("act", lambda e, ch=ch: e.activation(out=B_tm[:, ch, :], in_=psT[:, 0:128], func=AF.Copy), reads=[p.tps[7]], writes=[t_Btm])
        if full:
            for j in range(3):
                k.dma("sp", sp_sb, sprev[j, g], writes=[t_spsb])
                if j == 0:
                    k.op("dve", lambda e: e.tensor_copy(out=ent32, in_=sp_sb), reads=[t_spsb], writes=[t_ent32])
                else:
                    k.op("dve", lambda e, j=j, hs=hs: e.tensor_tensor(out=ent32.rearrange("p (h q) -> p h q", h=8), in0=ent32.rearrange("p (h q) -> p h q", h=8),
                                                              in1=tp_sb[:, j, hs].unsqueeze(2).to_broadcast([128, 8, 64]), op=ALU.mult),
                         reads=[t_ent32, t_tp], writes=[t_ent32])
                    k.op("dve", lambda e: e.tensor_tensor(out=ent32, in0=ent32, in1=sp_sb, op=ALU.add), reads=[t_ent32, t_spsb], writes=[t_ent32])
            k.op("act", lambda e: e.activation(out=entbf, in_=ent32, func=AF.Copy), reads=[t_ent32], writes=[t_entbf])
            wv_z, tw_z = p.wload(w_in, 0, KC, g * 512, 512)
            nw_ap, t_nw = nwg[g % 2]
            k.dma("sp", nw_ap, hpd[:, 3 * H + g * 512:3 * H + (g + 1) * 512], writes=[t_nw])
        else:
            k.op("dve", lambda e: e.memset(ent32, 0.0), writes=[t_ent32])
        for ch in range(NCH):
            tsl = slice(ch * Q, (ch + 1) * Q)
            tt = (ch * Q) // TT
            b2 = ch % 2

            def bc(vec):
                return vec[:, ch, hs].unsqueeze(2).to_broadcast([128, 8, 64])

            def v8(ap):
                return ap.rearrange("p (h q) -> p h q", h=8)
            xdd_ap, t_xdd = xdd[b2]
            k.op("dve", lambda e, xdd_ap=xdd_ap, ch=ch, b_=bc(dd_all): e.tensor_tensor(out=v8(xdd_ap), in0=v8(xs_tm[:, ch, :]), in1=b_, op=ALU.mult),
                 reads=[t_xstm, t_dd], writes=[t_xdd])
            if full:
                xdt_ap, t_xdt = xdt[b2]
                xsd_ap, t_xsd = xsd[b2]
                k.op("dve", lambda e, xdt_ap=xdt_ap, ch=ch, b_=bc(dt_all): e.tensor_tensor(out=v8(xdt_ap), in0=v8(xs_tm[:, ch, :]), in1=b_, op=ALU.mult),
                     reads=[t_xstm, t_dt], writes=[t_xdt])
                k.op("dve", lambda e, xsd_ap=xsd_ap, ch=ch, b_=dsk_b[:, hs].unsqueeze(2).to_broadcast([128, 8, 64]): e.tensor_tensor(
                    out=v8(xsd_ap), in0=v8(xs_tm[:, ch, :]), in1=b_, op=ALU.mult),
                     reads=[t_xstm, t_hp], writes=[t_xsd])
                ps_cb = p.ps[0][:, 0:128]
                k.op("pe", lambda e, ps_cb=ps_cb, tsl=tsl: e.matmul(ps_cb, lhsT=BT[:, tsl], rhs=CT[:, tsl], start=True, stop=True),
                     reads=[t_BT, t_CT], writes=[p.tps[0]])
                ad_ap, t_ad = adtri[b2]
                k.op("dve", lambda e, ad_ap=ad_ap, ch=ch, hs=hs: e.tensor_tensor(out=ad_ap, in0=adt_all[:, ch, hs].unsqueeze(2).to_broadcast([128, 8, 128]),
                                                                       in1=tri.unsqueeze(1).to_broadcast([128, 8, 128]), op=ALU.mult),
                     reads=[t_adt, t_cst], writes=[t_ad])
                ar_ap, t_ar = arg[b2]
                m_ap, t_m = Mt[b2]
                for half in range(2):
                    pss = p.ps[1 + half][:, 0:512]
                    hh = slice(4 * half, 4 * half + 4)
                    k.op("pe", lambda e, pss=pss, ad_ap=ad_ap, hh=hh: e.matmul(pss, lhsT=p.ones128, rhs=ad_ap[:, hh, :], start=True, stop=False),
                         reads=[p.t_ones128, t_ad], writes=[p.tps[1 + half]])
                    k.op("pe", lambda e, pss=pss, ch=ch, half=half, g=g: e.matmul(
                        pss, lhsT=ntri, rhs=adt_all[:, ch, 8 * g + 4 * half:8 * g + 4 * half + 4].unsqueeze(2).to_broadcast([128, 4, 128]),
                        start=False, stop=True), reads=[t_cst, t_adt], writes=[p.tps[1 + half]])
                    k.op("dve", lambda e, pss=pss, ar_ap=ar_ap, hh=hh: e.tensor_tensor(out=ar_ap[:, hh, :], in0=pss.rearrange("p (h l) -> p h l", h=4),
                                                                                   in1=negmask.unsqueeze(1).to_broadcast([128, 4, 128]), op=ALU.add),
                         reads=[p.tps[1 + half], t_cst], writes=[t_ar])
                k.op("act", lambda e, ar_ap=ar_ap: e.activation(out=ar_ap, in_=ar_ap, func=AF.Exp), reads=[t_ar], writes=[t_ar])
                k.op("dve", lambda e, m_ap=m_ap, ar_ap=ar_ap, ps_cb=ps_cb: e.tensor_tensor(out=m_ap, in0=ar_ap, in1=ps_cb.unsqueeze(1).to_broadcast([128, 8, 128]), op=ALU.mult),
                     reads=[t_ar, p.tps[0]], writes=[t_m])
                ps_d = p.ps[3][:, 0:512]
                k.op("pe", lambda e, ps_d=ps_d, xsd_ap=xsd_ap: e.matmul(ps_d, lhsT=identb, rhs=xsd_ap, start=True, stop=False),
                     reads=[t_identb, t_xsd], writes=[p.tps[3]])
                for h in range(8):
                    k.op("pe", lambda e, ps_d=ps_d, m_ap=m_ap, xdt_ap=xdt_ap, h=h: e.matmul(ps_d[:, h * 64:(h + 1) * 64], lhsT=m_ap[:, h, :], rhs=xdt_ap[:, h * 64:(h + 1) * 64],
                                                                                       start=False, stop=(h == 7)),
                         reads=[t_m, t_xdt], writes=[p.tps[3]])
                ps_o = p.ps[4][:, 0:512]
                k.op("pe", lambda e, ps_o=ps_o, tsl=tsl: e.matmul(ps_o, lhsT=CT[:, tsl], rhs=entbf, start=True, stop=True),
                     reads=[t_CT, t_entbf], writes=[p.tps[4]])
                ps_z = p.ps[5][:, 0:512]
                for kc in range(KC):
                    k.op("pe", lambda e, ps_z=ps_z, kc=kc, tsl=tsl, wv_z=wv_z: e.matmul(ps_z, lhsT=xbf[:, kc, tsl], rhs=wv_z[:, kc, :], start=(kc == 0), stop=(kc == KC - 1)),
                         reads=[tg_xbf[kc][tt], tw_z], writes=[p.tps[5]])
                y_ap, t_y = y32[b2]
                yt_ap, t_yt = ytmp[b2]
                sz_ap, t_sz = szb[b2]
                yn_ap, t_yn = ynb[b2]
                ss_ap, t_ss = ss[b2]
                k.op("dve", lambda e, yt_ap=yt_ap, ps_o=ps_o, b_=bc(eacs_all): e.tensor_tensor(out=v8(yt_ap), in0=v8(ps_o), in1=b_, op=ALU.mult),
                     reads=[p.tps[4], t_eacs], writes=[t_yt])
                k.op("dve", lambda e, y_ap=y_ap, yt_ap=yt_ap, ps_d=ps_d: e.tensor_tensor(out=y_ap, in0=yt_ap, in1=ps_d, op=ALU.add),
                     reads=[t_yt, p.tps[3]], writes=[t_y])
                k.op("act", lambda e, sz_ap=sz_ap, ps_z=ps_z: e.activation(out=sz_ap, in_=ps_z, func=AF.Silu), reads=[p.tps[5]], writes=[t_sz])
                k.op("dve", lambda e, y_ap=y_ap, sz_ap=sz_ap: e.tensor_tensor(out=y_ap, in0=y_ap, in1=sz_ap, op=ALU.mult), reads=[t_y, t_sz], writes=[t_y])
                k.op("act", lambda e, yt_ap=yt_ap, y_ap=y_ap, ss_ap=ss_ap: e.activation(out=yt_ap, in_=y_ap, func=AF.Square, accum_out=ss_ap),
                     reads=[t_y], writes=[t_yt, t_ss])
                k.op("act", lambda e, ss_ap=ss_ap: e.activation(out=ss_ap, in_=ss_ap, func=AF.Ln, scale=1.0 / 512.0, bias=p.eps128), reads=[t_ss, p.t_eps128], writes=[t_ss])
                k.op("act", lambda e, ss_ap=ss_ap: e.activation(out=ss_ap, in_=ss_ap, func=AF.Exp, scale=-0.5), reads=[t_ss], writes=[t_ss])
                k.op("dve", lambda e, yn_ap=yn_ap, y_ap=y_ap, ss_ap=ss_ap, nw_ap=nw_ap: e.scalar_tensor_tensor(out=yn_ap, in0=y_ap, scalar=ss_ap, in1=nw_ap,
                                                                                                     op0=ALU.mult, op1=ALU.mult),
                     reads=[t_y, t_ss, t_nw], writes=[t_yn])
                yst_ap, t_yst = ystg[b2]
                for j in range(4):
                    k.op("pe", lambda e, yn_ap=yn_ap, j=j: e.transpose(out=psT[:, 0:128], in_=yn_ap[:, j * 128:(j + 1) * 128], identity=identb),
                         reads=[t_yn, t_identb], writes=[p.tps[7]])
                    k.op("act", lambda e, j=j, yst_ap=yst_ap: e.activation(out=yst_ap[:, j, :], in_=psT[:, 0:128], func=AF.Copy),
                         reads=[p.tps[7]], writes=[t_yst])
                k.dma("sp", scr[g * 512:(g + 1) * 512, tsl].rearrange("(j p) t -> p j t", p=128), yst_ap, reads=[t_yst], writes=[t_scr[g]])
            ps_s = p.ps[6][:, 0:512]
            k.op("pe", lambda e, ps_s=ps_s, ch=ch, xdd_ap=xdd_ap: e.matmul(ps_s, lhsT=B_tm[:, ch, :], rhs=xdd_ap, start=True, stop=True),
                 reads=[t_Btm, t_xdd], writes=[p.tps[6]])
            k.op("dve", lambda e, b_=bc(etot_all): e.tensor_tensor(out=v8(ent32), in0=v8(ent32), in1=b_, op=ALU.mult), reads=[t_ent32, t_etot], writes=[t_ent32])
            k.op("dve", lambda e, ps_s=ps_s: e.tensor_tensor(out=ent32, in0=ent32, in1=ps_s, op=ALU.add), reads=[t_ent32, p.tps[6]], writes=[t_ent32])
            if full:
                k.op("act", lambda e: e.activation(out=entbf, in_=ent32, func=AF.Copy), reads=[t_ent32], writes=[t_entbf])
        if not full:
            k.dma("sp", s_out[g], ent32, reads=[t_ent32])
    if not full:
        return p
    p.release(mk0)
    out_proj_tail(p, scr, t_scr, xT, xbf, tg_xbf)
    return p


def out_proj_tail(p, scr, t_scr, xT, xbf, tg_xbf):
    c = p.c
    k = p.k
    D, KC, T_, TT = c.D, c.KC, c.T, c.TT
    DI = 2 * D
    w_out = p.din("w_out", [DI, D])
    x32, tg_x = p.alloc_act("x32", KC, F32)
    for kc in range(KC):
        k.dma("sp", x32[:, kc, :], xT[kc * 128:(kc + 1) * 128, :], writes=tg_x[kc])
    mk1 = p.mark()
    ynT, tg_yn = p.alloc_act("ynT", 2 * KC, BF16)
    for kc in range(2 * KC):
        k.dma("sp", ynT[:, kc, :], scr[kc * 128:(kc + 1) * 128, :], reads=list(t_scr), writes=tg_yn[kc])

    def evac_f(m, tt, pss, tpss):
        sl = slice(tt * TT, (tt + 1) * TT)
        k.op("dve", lambda e: e.scalar_tensor_tensor(out=x32[:, m, sl], in0=x32[:, m, sl], scalar=c.alpha, in1=pss[0], op0=ALU.mult, op1=ALU.add),
             reads=[tpss[0], tg_x[m][tt]], writes=[tg_x[m][tt]])
    p.proj_fm([(w_out, 0, 2 * KC, 0, ynT, tg_yn)], D, evac_f)
    p.release(mk1)
    p.tail(x32, tg_x, xbf, tg_xbf)


def prep_ssd(cfg, NG, conv_w, conv_b, dt_bias, a_log, d_skip, norm_w):
    DI = 2 * cfg.D
    H = DI // 64
    NXC = (DI + 2 * NG * 128) // 128
    cp = np.concatenate([conv_w, conv_b[None, :]], axis=0)
    convp = cp.reshape(5, NXC, 128).transpose(2, 1, 0)
    row = np.concatenate([dt_bias, a_log, d_skip, norm_w])[None, :]
    hp = np.broadcast_to(row, (128, row.shape[1]))
    return np.ascontiguousarray(convp.astype(np.float32)), np.ascontiguousarray(hp.astype(np.float32))


def ret_cst(cfg, q_idx, RH=8, DK=256):
    T_ = cfg.T
    f32 = np.float32
    theta = (1.0 / (f32(10000.0) ** np.linspace(0.0, 1.0, DK // 2, dtype=f32))).astype(f32)
    pos = (np.arange(T_, dtype=f32) + f32(q_idx * T_))
    ang = (theta[:, None] * pos[None, :]).astype(f32)
    cosT, sinT = np.cos(ang).astype(f32), np.sin(ang).astype(f32)
    lg = np.log1p(-np.exp2(-5.0 - np.arange(RH, dtype=f32))).astype(f32)
    n = np.arange(128, dtype=f32)
    diff = n[None, :] - n[:, None]
    sc = f32(DK ** -0.5)
    dmatT = np.where(diff[:, None, :] >= 0, np.exp(np.maximum(diff, 0)[:, None, :] * lg[None, :, None]), 0.0) * sc
    xi = np.exp((n[:, None] + 1.0) * lg[None, :])
    zeta = np.exp((127.0 - n[:, None]) * lg[None, :]) * sc
    cdec = np.broadcast_to(np.exp(128.0 * lg)[None, :], (128, RH))
    gT = np.broadcast_to(np.exp(float(T_) * lg)[None, :], (128, RH))
    small = np.concatenate([dmatT.reshape(128, RH * 128), xi, zeta, cdec, gT, np.eye(128, dtype=f32)], axis=1).astype(f32)
    return np.ascontiguousarray(np.concatenate([cosT, sinT], axis=1)), np.ascontiguousarray(small)


def build_ret(cfg, phase, RH=8):
    c = cfg
    p = Prog(cfg)
    k = p.k
    D, KC, T_, TT = c.D, c.KC, c.T, c.TT
    NT = c.NTT
    Q = 128
    NCH = T_ // Q
    DI = 2 * D
    DK, DV = 256, 512
    assert D == RH * DK and DI == RH * DV
    full = phase == "B"
    xT = p.din("xT", [D, T_])
    w_in = p.din("w_in", [D, 6 * D])
    xbf, tg_xbf = p.alloc_act("xbf", KC, BF16)
    for kc in range(KC):
        k.dma("pool", xbf[:, kc, :], xT[kc * 128:(kc + 1) * 128, :], writes=tg_xbf[kc])
    mk0 = p.mark()
    rope, t_rope = p.load_small("rope", [2 * T_])
    Ct, St = rope[:, 0:T_], rope[:, T_:2 * T_]
    cs, t_cs = p.load_small("rcst", [RH * 128 + 4 * RH + 128])
    dmatT = cs[:, 0:RH * 128].rearrange("p (h n) -> p h n", h=RH)
    o0 = RH * 128
    xi, zeta, cdec, gT = (cs[:, o0 + i * RH:o0 + (i + 1) * RH] for i in range(4))
    ident = cs[:, o0 + 4 * RH:o0 + 4 * RH + 128]
    identb, t_identb = p.alloc("identb", [128], BF16)
    k.op("dve", lambda e: e.tensor_copy(out=identb, in_=ident), reads=[t_cs], writes=[t_identb])
    if full:
        scr = p.dout("scr_yn", [DI, T_], BF16)
        t_scr = [T(f"scr{h}") for h in range(RH)]
        ystg = [p.alloc(f"ystg{i}", [4, 128], BF16) for i in range(2)]
    qT, t_qT = p.alloc("qT", [2, T_], BF16)
    kT, t_kT = p.alloc("kT", [2, T_], BF16)
    v_tm, t_v = p.alloc("v_tm", [NCH, 512], BF16)
    kz, t_kz = p.alloc("kz", [NCH, 256], BF16)
    st32 = [p.alloc(f"st32_{i}", [512], F32) for i in range(2)]
    stbf = [p.alloc(f"stbf_{i}", [512], BF16) for i in range(2)]
    ra = [p.alloc(f"ra{i}", [TT], F32) for i in range(2)]
    rb = [p.alloc(f"rb{i}", [TT], F32) for i in range(2)]
    if full:
        sg_tm, t_sg = p.alloc("sg_tm", [NCH, 512], BF16)
        scb = [p.alloc(f"scb{i}", [128], BF16) for i in range(2)]
        o32 = [p.alloc(f"o32_{i}", [512], F32) for i in range(2)]
        ot = [p.alloc(f"ot_{i}", [512], F32) for i in range(2)]
        yb = [p.alloc(f"yb_{i}", [512], BF16) for i in range(2)]
        stat = [p.alloc(f"stat_{i}", [4], F32) for i in range(2)]
        sprev = p.din("sprev", [3, RH, 2, 128, 512])
        sp_sb, t_spsb = p.alloc("sp_sb", [512], F32)
    else:
        s_out = p.dout("s_end", [RH, 2, 128, 512])
    cnt = 0
    for h in range(RH):
        apw, tw = p.wbuf[p.wq % 2]
        p.wq += 1
        vw = apw[:, 0:KC * 512].rearrange("p (a b) -> p a b", a=KC)
        which = [("k", D + h * DK, 256)]
        if full:
            which = [("q", h * DK, 0)] + which
        for nm, col, off in which:
            k.dma("pool", vw[:, :, off:off + 256], w_in[:, col:col + 256].rearrange("(kc p) n -> p kc n", p=128), writes=[tw])
        for nm, col, off in which:
            dstT, t_dst = (qT, t_qT) if nm == "q" else (kT, t_kT)
            for tt in range(NT):
                sl = slice(tt * TT, (tt + 1) * TT)
                pp = []
                for half in range(2):
                    ps = p.ps[half][:, 0:TT]
                    for kc in range(KC):
                        k.op("pe", lambda e, ps=ps, kc=kc, off=off, half=half, sl=sl, vw=vw: e.matmul(
                            ps, lhsT=vw[:, kc, off + half * 128:off + (half + 1) * 128], rhs=xbf[:, kc, sl], start=(kc == 0), stop=(kc == KC - 1)),
                            reads=[tw, tg_xbf[kc][tt]], writes=[p.tps[half]])
                    pp.append(ps)
                a_, t_a = ra[cnt % 2]
                b_, t_b = rb[cnt % 2]
                cnt += 1
                k.op("dve", lambda e, a_=a_, pp=pp, sl=sl: e.tensor_tensor(out=a_, in0=pp[0], in1=Ct[:, sl], op=ALU.mult), reads=[p.tps[0], t_rope], writes=[t_a])
                k.op("dve", lambda e, b_=b_, pp=pp, sl=sl: e.tensor_tensor(out=b_, in0=pp[1], in1=St[:, sl], op=ALU.mult), reads=[p.tps[1], t_rope], writes=[t_b])
                k.op("dve", lambda e, a_=a_, b_=b_, dstT=dstT, sl=sl: e.tensor_tensor(out=dstT[:, 0, sl], in0=a_, in1=b_, op=ALU.subtract), reads=[t_a, t_b], writes=[t_dst])
                k.op("dve", lambda e, a_=a_, pp=pp, sl=sl: e.tensor_tensor(out=a_, in0=pp[0], in1=St[:, sl], op=ALU.mult), reads=[p.tps[0], t_rope], writes=[t_a])
                k.op("dve", lambda e, b_=b_, pp=pp, sl=sl: e.tensor_tensor(out=b_, in0=pp[1], in1=Ct[:, sl], op=ALU.mult), reads=[p.tps[1], t_rope], writes=[t_b])
                k.op("dve", lambda e, a_=a_, b_=b_, dstT=dstT, sl=sl: e.tensor_tensor(out=dstT[:, 1, sl], in0=a_, in1=b_, op=ALU.add), reads=[t_a, t_b], writes=[t_dst])
        psT = p.ps[7].bitcast(BF16)
        for ch in range(NCH):
            tsl = slice(ch * Q, (ch + 1) * Q)
            for half in range(2):
                k.op("pe", lambda e, half=half, tsl=tsl: e.transpose(out=psT[:, 0:128], in_=kT[:, half, tsl], identity=identb), reads=[t_kT, t_identb], writes=[p.tps[7]])
                k.op("act", lambda e, ch=ch, half=half, h=h: e.activation(out=kz[:, ch, half * 128:(half + 1) * 128], in_=psT[:, 0:128], func=AF.Copy, scale=zeta[:, h:h + 1]),
                     reads=[p.tps[7], t_cs], writes=[t_kz])
        wv_v, tw_v = p.wload(w_in, 0, KC, 2 * D + h * DV, 512)
        for ch in range(NCH):
            tsl = slice(ch * Q, (ch + 1) * Q)
            tt = (ch * Q) // TT
            ps = p.ps[2 + ch % 2][:, 0:512]
            for kc in range(KC):
                k.op("pe", lambda e, ps=ps, kc=kc, tsl=tsl, wv_v=wv_v: e.matmul(ps, lhsT=xbf[:, kc, tsl], rhs=wv_v[:, kc, :], start=(kc == 0), stop=(kc == KC - 1)),
                     reads=[tg_xbf[kc][tt], tw_v], writes=[p.tps[2 + ch % 2]])
            k.op("act", lambda e, ps=ps, ch=ch: e.activation(out=v_tm[:, ch, :], in_=ps, func=AF.Copy), reads=[p.tps[2 + ch % 2]], writes=[t_v])
        if full:
            wv_g, tw_g = p.wload(w_in, 0, KC, 4 * D + h * DV, 512)
            for ch in range(NCH):
                tsl = slice(ch * Q, (ch + 1) * Q)
                tt = (ch * Q) // TT
                ps = p.ps[2 + ch % 2][:, 0:512]
                for kc in range(KC):
                    k.op("pe", lambda e, ps=ps, kc=kc, tsl=tsl, wv_g=wv_g: e.matmul(ps, lhsT=xbf[:, kc, tsl], rhs=wv_g[:, kc, :], start=(kc == 0), stop=(kc == KC - 1)),
                         reads=[tg_xbf[kc][tt], tw_g], writes=[p.tps[2 + ch % 2]])
                k.op("act", lambda e, ps=ps, ch=ch: e.activation(out=sg_tm[:, ch, :], in_=ps, func=AF.Silu), reads=[p.tps[2 + ch % 2]], writes=[t_sg])
        for half in range(2):
            s_ap, t_s = st32[half]
            if full:
                for j in range(3):
                    k.dma("sp", sp_sb, sprev[j, h, half], writes=[t_spsb])
                    if j == 0:
                        k.op("dve", lambda e, s_ap=s_ap: e.tensor_copy(out=s_ap, in_=sp_sb), reads=[t_spsb], writes=[t_s])
                    else:
                        k.op("dve", lambda e, s_ap=s_ap, h=h: e.scalar_tensor_tensor(out=s_ap, in0=s_ap, scalar=gT[:, h:h + 1], in1=sp_sb, op0=ALU.mult, op1=ALU.add),
                             reads=[t_s, t_spsb, t_cs], writes=[t_s])
                k.op("act", lambda e, s_ap=s_ap, half=half: e.activation(out=stbf[half][0], in_=s_ap, func=AF.Copy), reads=[t_s], writes=[stbf[half][1]])
            else:
                k.op("dve", lambda e, s_ap=s_ap: e.memset(s_ap, 0.0), writes=[t_s])
        for ch in range(NCH):
            tsl = slice(ch * Q, (ch + 1) * Q)
            tt = (ch * Q) // TT
            b2 = ch % 2
            if full:
                ps_sc = p.ps[4][:, 0:128]
                for half in range(2):
                    k.op("pe", lambda e, ps_sc=ps_sc, half=half, tsl=tsl: e.matmul(ps_sc, lhsT=kT[:, half, tsl], rhs=qT[:, half, tsl], start=(half == 0), stop=(half == 1)),
                         reads=[t_kT, t_qT], writes=[p.tps[4]])
                sc_ap, t_sc = scb[b2]
                k.op("dve", lambda e, sc_ap=sc_ap, ps_sc=ps_sc, h=h: e.tensor_tensor(out=sc_ap, in0=ps_sc, in1=dmatT[:, h, :], op=ALU.mult), reads=[p.tps[4], t_cs], writes=[t_sc])
                ps_in = p.ps[5][:, 0:512]
                k.op("pe", lambda e, ps_in=ps_in, sc_ap=sc_ap, ch=ch: e.matmul(ps_in, lhsT=sc_ap, rhs=v_tm[:, ch, :], start=True, stop=True), reads=[t_sc, t_v], writes=[p.tps[5]])
                ps_cr = p.ps[6][:, 0:512]
                for half in range(2):
                    k.op("pe", lambda e, ps_cr=ps_cr, half=half, tsl=tsl: e.matmul(ps_cr, lhsT=qT[:, half, tsl], rhs=stbf[half][0], start=(half == 0), stop=(half == 1)),
                         reads=[t_qT, stbf[half][1]], writes=[p.tps[6]])
                o_ap, t_o = o32[b2]
                ot_ap, t_ot = ot[b2]
                y_ap, t_y = yb[b2]
                st_ap, t_st = stat[b2]
                k.op("act", lambda e, ot_ap=ot_ap, ps_cr=ps_cr, h=h: e.activation(out=ot_ap, in_=ps_cr, func=AF.Copy, scale=xi[:, h:h + 1]), reads=[p.tps[6], t_cs], writes=[t_ot])
                k.op("dve", lambda e, o_ap=o_ap, ot_ap=ot_ap, ps_in=ps_in: e.tensor_tensor(out=o_ap, in0=ot_ap, in1=ps_in, op=ALU.add), reads=[t_ot, p.tps[5]], writes=[t_o])
                k.op("act", lambda e, ot_ap=ot_ap, o_ap=o_ap, st_ap=st_ap: e.activation(out=ot_ap, in_=o_ap, func=AF.Copy, accum_out=st_ap[:, 0:1]), reads=[t_o], writes=[t_ot, t_st])
                k.op("act", lambda e, ot_ap=ot_ap, o_ap=o_ap, st_ap=st_ap: e.activation(out=ot_ap, in_=o_ap, func=AF.Square, accum_out=st_ap[:, 1:2]), reads=[t_o, t_st], writes=[t_ot, t_st])
                k.op("dve", lambda e, st_ap=st_ap: e.tensor_scalar(out=st_ap[:, 0:2], in0=st_ap[:, 0:2], scalar1=1.0 / DV, scalar2=None, op0=ALU.mult), reads=[t_st], writes=[t_st])
                k.op("dve", lambda e, st_ap=st_ap: e.tensor_tensor(out=st_ap[:, 2:3], in0=st_ap[:, 0:1], in1=st_ap[:, 0:1], op=ALU.mult), reads=[t_st], writes=[t_st])
                k.op("dve", lambda e, st_ap=st_ap: e.tensor_tensor(out=st_ap[:, 1:2], in0=st_ap[:, 1:2], in1=st_ap[:, 2:3], op=ALU.subtract), reads=[t_st], writes=[t_st])
                k.op("act", lambda e, st_ap=st_ap: e.activation(out=st_ap[:, 1:2], in_=st_ap[:, 1:2], func=AF.Ln, bias=p.eps128), reads=[t_st, p.t_eps128], writes=[t_st])
                k.op("act", lambda e, st_ap=st_ap: e.activation(out=st_ap[:, 1:2], in_=st_ap[:, 1:2], func=AF.Exp, scale=-0.5), reads=[t_st], writes=[t_st])
                k.op("dve", lambda e, st_ap=st_ap: e.scalar_tensor_tensor(out=st_ap[:, 3:4], in0=st_ap[:, 0:1], scalar=-1.0, in1=st_ap[:, 1:2], op0=ALU.mult, op1=ALU.mult),
                     reads=[t_st], writes=[t_st])
                k.op("act", lambda e, ot_ap=ot_ap, o_ap=o_ap, st_ap=st_ap: e.activation(out=ot_ap, in_=o_ap, func=AF.Identity, scale=st_ap[:, 1:2], bias=st_ap[:, 3:4]),
                     reads=[t_o, t_st], writes=[t_ot])
                k.op("dve", lambda e, y_ap=y_ap, ot_ap=ot_ap, ch=ch: e.tensor_tensor(out=y_ap, in0=ot_ap, in1=sg_tm[:, ch, :], op=ALU.mult), reads=[t_ot, t_sg], writes=[t_y])
                yst_ap, t_yst = ystg[b2]
                for j in range(4):
                    k.op("pe", lambda e, y_ap=y_ap, j=j: e.transpose(out=psT[:, 0:128], in_=y_ap[:, j * 128:(j + 1) * 128], identity=identb), reads=[t_y, t_identb], writes=[p.tps[7]])
                    k.op("act", lambda e, j=j, yst_ap=yst_ap: e.activation(out=yst_ap[:, j, :], in_=psT[:, 0:128], func=AF.Copy), reads=[p.tps[7]], writes=[t_yst])
                k.dma("sp", scr[h * 512:(h + 1) * 512, tsl].rearrange("(j p) t -> p j t", p=128), yst_ap, reads=[t_yst], writes=[t_scr[h]])
            for half in range(2):
                s_ap, t_s = st32[half]
                ps_s = p.ps[half][:, 0:512]
                k.op("pe", lambda e, ps_s=ps_s, ch=ch, half=half: e.matmul(ps_s, lhsT=kz[:, ch, half * 128:(half + 1) * 128], rhs=v_tm[:, ch, :], start=True, stop=True),
                     reads=[t_kz, t_v], writes=[p.tps[half]])
                k.op("dve", lambda e, s_ap=s_ap, ps_s=ps_s, h=h: e.scalar_tensor_tensor(out=s_ap, in0=s_ap, scalar=cdec[:, h:h + 1], in1=ps_s, op0=ALU.mult, op1=ALU.add),
                     reads=[t_s, p.tps[half], t_cs], writes=[t_s])
                if full:
                    k.op("act", lambda e, s_ap=s_ap, half=half: e.activation(out=stbf[half][0], in_=s_ap, func=AF.Copy), reads=[t_s], writes=[stbf[half][1]])
        if not full:
            for half in range(2):
                k.dma("sp", s_out[h, half], st32[half][0], reads=[st32[half][1]])
    if not full:
        return p
    p.release(mk0)
    out_proj_tail(p, scr, t_scr, xT, xbf, tg_xbf)
    return p


_PROG_CACHE = {}


def _get_prog(key, fn):
    if key not in _PROG_CACHE:
        _PROG_CACHE[key] = fn().finish()
    return _PROG_CACHE[key]


def _hw_runner(nc, in_maps, out_names):
    res = run_bass_kernel_spmd(nc, in_maps, core_ids=list(range(len(in_maps))))
    return [{k: np.asarray(r[k]) for k in out_names} for r in res.results]


def run_model(inputs, cfg, NB, NQ, NG, RH, runner, depth=4):
    f32 = np.float32
    ncore = NB * NQ
    T_ = cfg.T
    x = np.asarray(inputs["x"], dtype=f32)
    xT = [np.ascontiguousarray(x[c // NQ, (c % NQ) * T_:((c % NQ) + 1) * T_].T) for c in range(ncore)]

    def prev_slots(c, states, zero):
        out = []
        for j in range(3):
            src = (c % NQ) - 3 + j
            out.append(states[(c // NQ) * NQ + src] if src >= 0 else zero)
        return out

    for i in range(depth):
        kind, j = i % 3, i // 3
        tail_in = {"w1": np.ascontiguousarray(inputs["mlp_w1"][i]), "w2": np.ascontiguousarray(inputs["mlp_w2"][i]),
                   "lnp": prep_lnp(cfg, inputs["ln1_g"][i], inputs["ln1_b"][i], inputs["ln2_g"][i], inputs["ln2_b"][i])}
        if kind == 0:
            s5a, s5b = prep_s5(cfg, *(np.asarray(inputs[n][j], dtype=f32) for n in
                                      ("s5_lam_re", "s5_lam_im", "s5_log_dt", "s5_b_re", "s5_b_im", "s5_c_re", "s5_c_im", "s5_d")))
            cst = s5_cst(cfg)
            w_in = np.ascontiguousarray(inputs["s5_w_in"][j])
            base = [{"xT": xT[c], "w_in": w_in, "cst": cst, "s5a": s5a, "s5b": s5b} for c in range(ncore)]
            ncA = _get_prog(("s5A", id(cfg)), lambda: build_s5(cfg, "A"))
            resA = runner(ncA, base, ["s_end"])
            S = [r["s_end"] for r in resA]
            zero = np.zeros_like(S[0])
            ins = []
            for c in range(ncore):
                sl = prev_slots(c, S, zero)
                sp = np.stack(sl + [np.concatenate([s[64:], s[:64]], axis=0) for s in sl], axis=1)
                d = dict(base[c])
                d.update(tail_in)
                d.update({"sprev": np.ascontiguousarray(sp), "w_out": np.ascontiguousarray(inputs["s5_w_out"][j]),
                          "w_gate": np.ascontiguousarray(inputs["s5_w_gate"][j])})
                ins.append(d)
            ncB = _get_prog(("s5B", id(cfg)), lambda: build_s5(cfg, "B"))
            resB = runner(ncB, ins, ["xT_out"])
        elif kind == 1:
            convp, hp = prep_ssd(cfg, NG, *(np.asarray(inputs[n][j], dtype=f32) for n in
                                            ("ssd_conv_w", "ssd_conv_b", "ssd_dt_bias", "ssd_a_log", "ssd_d", "ssd_norm_w")))
            cst = ssd_cst()
            w_in = np.ascontiguousarray(inputs["ssd_w_in"][j])
            base = []
            for c in range(ncore):
                xp = np.zeros((cfg.D, 4), f32)
                if c % NQ > 0:
                    xp[:, 1:4] = xT[c - 1][:, -3:]
                base.append({"xT": xT[c], "xprevT": xp, "w_in": w_in, "cst": cst, "hp": hp, "convp": convp})
            ncA = _get_prog(("ssdA", id(cfg)), lambda: build_ssd(cfg, "A", NG))
            resA = runner(ncA, base, ["s_end", "tot"])
            S = [r["s_end"] for r in resA]
            TOT = [r["tot"] for r in resA]
            ins = []
            for c in range(ncore):
                sp = np.stack(prev_slots(c, S, np.zeros_like(S[0])), axis=0)
                tp = np.stack(prev_slots(c, TOT, np.zeros_like(TOT[0])), axis=1)
                d = dict(base[c])
                d.update(tail_in)
                d.update({"sprev": np.ascontiguousarray(sp), "totprev": np.ascontiguousarray(tp), "w_out": np.ascontiguousarray(inputs["ssd_w_out"][j])})
                ins.append(d)
            ncB = _get_prog(("ssdB", id(cfg)), lambda: build_ssd(cfg, "B", NG))
            resB = runner(ncB, ins, ["xT_out"])
        else:
            w_in = np.ascontiguousarray(inputs["ret_w_in"][j])
            csts = [ret_cst(cfg, q, RH, cfg.D // RH) for q in range(NQ)]
            base = [{"xT": xT[c], "w_in": w_in, "rope": csts[c % NQ][0], "rcst": csts[c % NQ][1]} for c in range(ncore)]
            ncA = _get_prog(("retA", id(cfg)), lambda: build_ret(cfg, "A", RH))
            resA = runner(ncA, base, ["s_end"])
            S = [r["s_end"] for r in resA]
            ins = []
            for c in range(ncore):
                sp = np.stack(prev_slots(c, S, np.zeros_like(S[0])), axis=0)
                d = dict(base[c])
                d.update(tail_in)
                d.update({"sprev": np.ascontiguousarray(sp), "w_out": np.ascontiguousarray(inputs["ret_w_out"][j])})
                ins.append(d)
            ncB = _get_prog(("retB", id(cfg)), lambda: build_ret(cfg, "B", RH))
            resB = runner(ncB, ins, ["xT_out"])
        xT = [np.ascontiguousarray(r["xT_out"]) for r in resB]
    out = np.empty_like(x)
    for c in range(ncore):
        out[c // NQ, (c % NQ) * T_:((c % NQ) + 1) * T_] = xT[c].T
    return out


_CFG = Cfg()


def kernel(**inputs):
    return run_model(inputs, _CFG, NB=2, NQ=4, NG=8, RH=8, runner=_hw_runner)
```
